# Optimizing a Trainium2 kernel written in Bass

```python
import math
import jax, jax.numpy as jnp
from jax import lax
import numpy as np

D_MODEL = 1024
BATCH = 2
SEQ = 8192
DEPTH = 4
DEC_BATCH = 128
DEC_SEQ = 8
PAST_LEN = 2048
PAGE_SIZE = 128

GDN_HEADS = 4
GDN_DK = 128
GDN_DV = 128
GDN_QKV = GDN_HEADS * (2 * GDN_DK + GDN_DV)
CONV_W = 4
GDN_CHUNK = 64
NSA_HEADS = 8
NSA_KV_HEADS = 2
NSA_HD = 64
NSA_REP = NSA_HEADS // NSA_KV_HEADS
CMP_LEN = 32
CMP_STRIDE = 16
SEL_BLOCK = 64
N_SEL = 16
WINDOW = 512
Q_BLOCK = 128
SAMPLE_Q_BLOCK = 1
NSA_KV_W = 2 * NSA_KV_HEADS * NSA_HD
MIX_W = GDN_HEADS * GDN_DV + NSA_HEADS * NSA_HD
D_FF = (8 * D_MODEL + 3 * 256 - 1) // (3 * 256) * 256
IN_SIZES = (GDN_QKV, GDN_HEADS, GDN_HEADS, GDN_HEADS * GDN_DV,
            NSA_HEADS * NSA_HD, NSA_KV_W, NSA_KV_W, NSA_KV_W, 3 * NSA_HEADS)
IN_SPLITS = tuple(int(s) for s in np.cumsum(IN_SIZES)[:-1])
N_IN = sum(IN_SIZES)
NEG_INF = -1e30
FORCE_BONUS = 1e4
EPS = 1e-6

kernel_name = "hybrid_gdn_nsa_decode_step"


def rmsnorm(x, g):
    xf = x.astype(jnp.float32)
    y = xf * lax.rsqrt(jnp.mean(xf * xf, axis=-1, keepdims=True) + EPS)
    return (y * g.astype(jnp.float32)).astype(x.dtype)


def l2norm(x):
    xf = x.astype(jnp.float32)
    return (xf * lax.rsqrt(jnp.sum(xf * xf, axis=-1, keepdims=True) + EPS)).astype(x.dtype)


def alibi_slopes(n):
    return jnp.asarray([2.0 ** (-8.0 * (h + 1) / n) for h in range(n)], jnp.float32)


def causal_conv(u, buf, w):
    up = jnp.concatenate([buf.astype(u.dtype), u], axis=1)
    T = u.shape[1]
    y = sum(up[:, j:j + T] * w[j] for j in range(CONV_W))
    return jax.nn.silu(y), up[:, -(CONV_W - 1):]


def gated_delta_rule(q, k, v, beta, g, s0):
    f32 = jnp.float32
    B, T, H, DK = q.shape
    DV = v.shape[-1]
    C = min(GDN_CHUNK, T)
    n = -(-T // C)
    pad = n * C - T

    def prep(a):
        a = jnp.pad(a.astype(f32), [(0, 0), (0, pad)] + [(0, 0)] * (a.ndim - 2))
        return jnp.swapaxes(a.reshape((B, n, C) + a.shape[2:]), 2, 3)

    qc = prep(q) * DK ** -0.5
    kc, vc, bc, gc = prep(k), prep(v), prep(beta), prep(g)
    dcy = jnp.cumsum(gc, axis=-1)
    lower = jnp.tril(jnp.ones((C, C), bool))
    strict = jnp.tril(jnp.ones((C, C), bool), -1)
    dmat = jnp.where(lower, jnp.exp(jnp.where(lower, dcy[..., :, None] - dcy[..., None, :], 0.0)), 0.0)
    kb = kc * bc[..., None]
    a_mat = jnp.where(strict, jnp.einsum('bnhid,bnhjd->bnhij', kb, kc) * dmat, 0.0)
    rhs = jnp.concatenate([vc * bc[..., None], kb * jnp.exp(dcy)[..., None]], axis=-1)
    sol = lax.linalg.triangular_solve(jnp.eye(C, dtype=f32) + a_mat, rhs, left_side=True,
                                      lower=True, unit_diagonal=True)
    u, w = sol[..., :DV], sol[..., DV:]
    attn = jnp.where(lower, jnp.einsum('bnhid,bnhjd->bnhij', qc, kc) * dmat, 0.0)
    qd = qc * jnp.exp(dcy)[..., None]
    dlast = dcy[..., -1:]
    kd = kc * jnp.exp(dlast - dcy)[..., None]
    glast = jnp.exp(dlast[..., 0])

    def step(s, xs):
        u_i, w_i, qd_i, kd_i, at_i, gl_i = xs
        v_new = u_i - jnp.einsum('bhck,bhkv->bhcv', w_i, s)
        o = jnp.einsum('bhck,bhkv->bhcv', qd_i, s) + jnp.einsum('bhij,bhjv->bhiv', at_i, v_new)
        s = s * gl_i[..., None, None] + jnp.einsum('bhck,bhcv->bhkv', kd_i, v_new)
        return s, o

    xs = tuple(jnp.moveaxis(a, 1, 0) for a in (u, w, qd, kd, attn, glast))
    s_fin, o = lax.scan(step, s0.astype(f32), xs)
    o = jnp.swapaxes(jnp.moveaxis(o, 0, 1), 2, 3).reshape(B, n * C, H, DV)[:, :T]
    return o.astype(q.dtype), s_fin.astype(s0.dtype)


def gdn_mixer(qkv_raw, b_logit, a_logit, gate, conv_buf, s0, conv_w, a_log, dt_bias, norm_g):
    B, T, _ = qkv_raw.shape
    qkv, new_buf = causal_conv(qkv_raw, conv_buf, conv_w)
    q, k, v = jnp.split(qkv, [GDN_HEADS * GDN_DK, 2 * GDN_HEADS * GDN_DK], axis=-1)
    q = l2norm(q.reshape(B, T, GDN_HEADS, GDN_DK))
    k = l2norm(k.reshape(B, T, GDN_HEADS, GDN_DK))
    v = v.reshape(B, T, GDN_HEADS, GDN_DV)
    beta = jax.nn.sigmoid(b_logit.astype(jnp.float32))
    g = -jnp.exp(a_log.astype(jnp.float32)) * jax.nn.softplus(a_logit.astype(jnp.float32) + dt_bias.astype(jnp.float32))
    o, s_new = gated_delta_rule(q, k, v, beta, g, s0)
    o = rmsnorm(o, norm_g) * jax.nn.silu(gate.reshape(B, T, GDN_HEADS, GDN_DV))
    return o.reshape(B, T, GDN_HEADS * GDN_DV), s_new, new_buf


def compress_blocks(kv, cmp_w, cmp_pe):
    B, T = kv.shape[:2]
    r = CMP_LEN // CMP_STRIDE
    n_sub = -(-T // CMP_STRIDE)
    kvp = jnp.pad(kv, ((0, 0), (0, n_sub * CMP_STRIDE - T), (0, 0), (0, 0), (0, 0)))
    sub = kvp.reshape((B, n_sub, CMP_STRIDE) + kv.shape[2:])
    nc = n_sub - r + 1
    blk = jnp.concatenate([sub[:, j:j + nc] for j in range(r)], axis=2)
    ckv = jnp.einsum('bnlcgd,lcde->bncge', blk + cmp_pe[:, :, None, :].astype(blk.dtype), cmp_w)
    c_start = jnp.arange(nc, dtype=jnp.int32) * CMP_STRIDE
    c_end = c_start + (CMP_LEN - 1)
    c_ctr = c_start.astype(jnp.float32) + 0.5 * (CMP_LEN - 1)
    return ckv, c_end, c_ctr


def sel_blocks(kv):
    B, T = kv.shape[:2]
    ns = -(-T // SEL_BLOCK)
    kvp = jnp.pad(kv, ((0, 0), (0, ns * SEL_BLOCK - T), (0, 0), (0, 0), (0, 0)))
    return kvp.reshape((B, ns, SEL_BLOCK) + kv.shape[2:])


def nsa_attend(q, qpos, ckv, c_end, c_ctr, skv, wkv, wpos, wlen, qb):
    f32 = jnp.float32
    B, T, H, D = q.shape
    G, R = NSA_KV_HEADS, NSA_REP
    nb = T // qb
    nc, ns = ckv.shape[1], skv.shape[1]
    spb = SEL_BLOCK // CMP_STRIDE
    n_sel = min(N_SEL, ns)
    scale = D ** -0.5
    slopes = alibi_slopes(H).reshape(G, R)
    kc, vc = ckv[:, :, 0], ckv[:, :, 1]
    sk = jnp.transpose(skv[:, :, :, 0], (0, 3, 1, 2, 4))
    sv = jnp.transpose(skv[:, :, :, 1], (0, 3, 1, 2, 4))
    b_ix = jnp.arange(B)[:, None, None, None]
    g_ix = jnp.arange(G)[None, :, None, None]
    blk_ix = jnp.arange(ns, dtype=jnp.int32)

    def block(args):
        qi, pi, st = args
        qg = qi.reshape(B, qb, G, R, D)
        pf = pi.astype(f32)
        s_c = jnp.einsum('bqgrd,bngd->bgrqn', qg, kc, preferred_element_type=f32) * scale
        s_c = s_c - slopes[:, :, None, None] * (pf[:, None] - c_ctr[None, :])
        ok_c = c_end[None, :] <= pi[:, None]
        p_c = jax.nn.softmax(jnp.where(ok_c, s_c, NEG_INF), axis=-1) * ok_c
        o_c = jnp.einsum('bgrqn,bngd->bqgrd', p_c.astype(vc.dtype), vc)
        imp = jnp.pad(p_c.sum(axis=2), ((0, 0), (0, 0), (0, 0), (0, ns * spb - nc)))
        imp = imp.reshape(B, G, qb, ns, spb).sum(-1)
        cur = (pi // SEL_BLOCK)[:, None]
        forced = (blk_ix == 0) | (blk_ix == cur) | (blk_ix == cur - 1)
        score = jnp.where(blk_ix <= cur, imp + jnp.where(forced, FORCE_BONUS, 0.0), NEG_INF)
        _, idx = lax.top_k(score, n_sel)
        ks_ = sk[b_ix, g_ix, idx]
        vs_ = sv[b_ix, g_ix, idx]
        dist_s = pi[:, None, None] - (idx[..., None] * SEL_BLOCK + jnp.arange(SEL_BLOCK, dtype=jnp.int32))
        s_s = jnp.einsum('bqgrd,bgqnsd->bgrqns', qg, ks_, preferred_element_type=f32) * scale
        s_s = s_s - slopes[:, :, None, None, None] * dist_s[:, :, None].astype(f32)
        s_s = jnp.where(dist_s[:, :, None] >= 0, s_s, NEG_INF).reshape(B, G, R, qb, n_sel * SEL_BLOCK)
        p_s = jax.nn.softmax(s_s, axis=-1).reshape(B, G, R, qb, n_sel, SEL_BLOCK)
        o_s = jnp.einsum('bgrqns,bgqnsd->bqgrd', p_s.astype(vs_.dtype), vs_)
        kw = lax.dynamic_slice_in_dim(wkv, st, qb + wlen, axis=1)
        pw = lax.dynamic_slice_in_dim(wpos, st, qb + wlen)
        dist_w = pi[:, None] - pw[None, :]
        ok_w = (dist_w >= 0) & (dist_w < WINDOW) & (pw >= 0)[None, :]
        s_w = jnp.einsum('bqgrd,bkgd->bgrqk', qg, kw[:, :, 0], preferred_element_type=f32) * scale
        s_w = s_w - slopes[:, :, None, None] * dist_w.astype(f32)
        p_w = jax.nn.softmax(jnp.where(ok_w, s_w, NEG_INF), axis=-1)
        o_w = jnp.einsum('bgrqk,bkgd->bqgrd', p_w.astype(kw.dtype), kw[:, :, 1])
        return jnp.stack([o_c, o_s, o_w], axis=-2)

    out = lax.map(block, (jnp.swapaxes(q.reshape(B, nb, qb, H, D), 0, 1),
                          qpos.reshape(nb, qb),
                          jnp.arange(nb, dtype=jnp.int32) * qb))
    return jnp.swapaxes(out, 0, 1).reshape(B, T, H, 3, D)


def trunk_layer(x, past_len, cmp_past, slc_past, win_buf, win_start, keep, conv_buf, s0, qb,
                g_mix, w_in, conv_w, a_log, dt_bias, gdn_g, cmp_w, cmp_pe, nsa_g,
                w_out, g_ffn, w_ffn_in, w_ffn_out):
    B, T, _ = x.shape
    h = rmsnorm(x, g_mix)
    proj = h @ w_in
    qkv_raw, b_logit, a_logit, gate, nq, ncmp, nslc, nwin, ngate = jnp.split(proj, IN_SPLITS, axis=-1)
    o_gdn, s_new, conv_new = gdn_mixer(qkv_raw, b_logit, a_logit, gate, conv_buf, s0,
                                       conv_w, a_log, dt_bias, gdn_g)
    kvr = lambda a: a.reshape(B, T, 2, NSA_KV_HEADS, NSA_HD)
    ncmp, nslc, nwin = kvr(ncmp), kvr(nslc), kvr(nwin)
    cmp_all = jnp.concatenate([cmp_past.astype(x.dtype), ncmp], axis=1)
    slc_all = jnp.concatenate([slc_past.astype(x.dtype), nslc], axis=1)
    wkv = jnp.concatenate([win_buf.astype(x.dtype), nwin], axis=1)
    wpos = win_start + jnp.arange(wkv.shape[1], dtype=jnp.int32)
    qpos = past_len + jnp.arange(T, dtype=jnp.int32)
    ckv, c_end, c_ctr = compress_blocks(cmp_all, cmp_w, cmp_pe)
    o_br = nsa_attend(nq.reshape(B, T, NSA_HEADS, NSA_HD), qpos, ckv, c_end, c_ctr,
                      sel_blocks(slc_all), wkv, wpos, win_buf.shape[1], qb)
    gates = jax.nn.sigmoid(ngate.astype(jnp.float32)).reshape(B, T, NSA_HEADS, 3)
    o_nsa = jnp.einsum('bthc,bthcd->bthd', gates.astype(o_br.dtype), o_br)
    o_nsa = rmsnorm(o_nsa, nsa_g).reshape(B, T, NSA_HEADS * NSA_HD)
    x = x + jnp.concatenate([o_gdn, o_nsa], axis=-1) @ w_out
    h = rmsnorm(x, g_ffn)
    gt, up = jnp.split(h @ w_ffn_in, 2, axis=-1)
    x = x + (jax.nn.silu(gt) * up) @ w_ffn_out
    return x, ncmp, nslc, wkv[:, -keep:], s_new, conv_new


def gather_pages(pool, page_table):
    g = pool[page_table]
    return g.reshape((page_table.shape[0], page_table.shape[1] * pool.shape[1]) + pool.shape[2:])


def setup_inputs(seed: int = 0) -> dict:
    key = jax.random.key(seed)
    ks = jax.random.split(key, 24)
    f32 = jnp.float32
    n_pages = PAST_LEN // PAGE_SIZE
    n_pool = (5 * DEC_BATCH * n_pages + 3) // 4
    w_buf = min(WINDOW, PAST_LEN)
    kv_row = (2, NSA_KV_HEADS, NSA_HD)

    def nrm(k, shape, s):
        return s * jax.random.normal(k, shape, f32)

    x_prompt = nrm(ks[0], (BATCH, SEQ, D_MODEL), 1.0)
    x_sample = nrm(ks[1], (DEC_BATCH, DEC_SEQ, D_MODEL), 1.0)
    cache_cmp_kv = nrm(ks[2], (DEPTH, n_pool, PAGE_SIZE) + kv_row, 1.0)
    cache_slc_kv = nrm(ks[3], (DEPTH, n_pool, PAGE_SIZE) + kv_row, 1.0)
    page_table = jax.random.permutation(ks[4], n_pool)[:DEC_BATCH * n_pages].reshape(DEC_BATCH, n_pages).astype(jnp.int32)
    cache_win_kv = nrm(ks[5], (DEPTH, DEC_BATCH, w_buf) + kv_row, 1.0)
    state_gdn = nrm(ks[6], (DEPTH, DEC_BATCH, GDN_HEADS, GDN_DK, GDN_DV), 0.05)
    state_conv = nrm(ks[7], (DEPTH, DEC_BATCH, CONV_W - 1, GDN_QKV), 1.0)
    norm_mix = 1.0 + nrm(ks[8], (DEPTH, D_MODEL), 0.02)
    w_in = nrm(ks[9], (DEPTH, D_MODEL, N_IN), D_MODEL ** -0.5)
    conv_w = nrm(ks[10], (DEPTH, CONV_W, GDN_QKV), CONV_W ** -0.5)
    gdn_a_log = jnp.log(jax.random.uniform(ks[11], (DEPTH, GDN_HEADS), f32, minval=1.0, maxval=16.0))
    dt = jnp.exp(jax.random.uniform(ks[12], (DEPTH, GDN_HEADS), f32,
                                    minval=math.log(1e-3), maxval=math.log(1e-1)))
    gdn_dt_bias = dt + jnp.log(-jnp.expm1(-dt))
    gdn_norm = 1.0 + nrm(ks[13], (DEPTH, GDN_DV), 0.02)
    nsa_cmp_w = nrm(ks[14], (DEPTH, CMP_LEN, 2, NSA_HD, NSA_HD), (CMP_LEN * NSA_HD) ** -0.5)
    nsa_cmp_pe = nrm(ks[15], (DEPTH, CMP_LEN, 2, NSA_HD), 0.02)
    nsa_norm = 1.0 + nrm(ks[16], (DEPTH, NSA_HD), 0.02)
    w_out = nrm(ks[17], (DEPTH, MIX_W, D_MODEL), MIX_W ** -0.5)
    norm_ffn = 1.0 + nrm(ks[18], (DEPTH, D_MODEL), 0.02)
    w_ffn_in = nrm(ks[19], (DEPTH, D_MODEL, 2 * D_FF), D_MODEL ** -0.5)
    w_ffn_out = nrm(ks[20], (DEPTH, D_FF, D_MODEL), D_FF ** -0.5)
    norm_final = 1.0 + nrm(ks[21], (D_MODEL,), 0.02)
    return {"x_prompt": x_prompt, "x_sample": x_sample,
            "cache_cmp_kv": cache_cmp_kv, "cache_slc_kv": cache_slc_kv, "page_table": page_table,
            "cache_win_kv": cache_win_kv, "state_gdn": state_gdn, "state_conv": state_conv,
            "norm_mix": norm_mix, "w_in": w_in, "conv_w": conv_w, "gdn_a_log": gdn_a_log,
            "gdn_dt_bias": gdn_dt_bias, "gdn_norm": gdn_norm, "nsa_cmp_w": nsa_cmp_w,
            "nsa_cmp_pe": nsa_cmp_pe, "nsa_norm": nsa_norm, "w_out": w_out, "norm_ffn": norm_ffn,
            "w_ffn_in": w_ffn_in, "w_ffn_out": w_ffn_out, "norm_final": norm_final}


def reference(x_prompt, x_sample, cache_cmp_kv, cache_slc_kv, page_table, cache_win_kv, state_gdn,
              state_conv, norm_mix, w_in, conv_w, gdn_a_log, gdn_dt_bias, gdn_norm, nsa_cmp_w,
              nsa_cmp_pe, nsa_norm, w_out, norm_ffn, w_ffn_in, w_ffn_out, norm_final):
    bp, tp = x_prompt.shape[0], x_prompt.shape[1]
    db = x_sample.shape[0]
    w_buf = cache_win_kv.shape[2]
    kv_row = (2, NSA_KV_HEADS, NSA_HD)
    yp, ys = x_prompt, x_sample
    cmp_p, cmp_s, slc_p, slc_s, win_p, win_s, gdn_p, gdn_s, conv_p, conv_s = ([] for _ in range(10))
    for l in range(DEPTH):
        wts = (norm_mix[l], w_in[l], conv_w[l], gdn_a_log[l], gdn_dt_bias[l], gdn_norm[l],
               nsa_cmp_w[l], nsa_cmp_pe[l], nsa_norm[l], w_out[l], norm_ffn[l], w_ffn_in[l], w_ffn_out[l])
        empty = jnp.zeros((bp, 0) + kv_row, x_prompt.dtype)
        yp, c1, s1, w1, g1, v1 = trunk_layer(
            yp, 0, empty, empty, jnp.zeros((bp, WINDOW) + kv_row, x_prompt.dtype), -WINDOW,
            min(WINDOW, tp), jnp.zeros((bp, CONV_W - 1, GDN_QKV), x_prompt.dtype),
            jnp.zeros((bp, GDN_HEADS, GDN_DK, GDN_DV), x_prompt.dtype), Q_BLOCK, *wts)
        cmp_past = gather_pages(cache_cmp_kv[l], page_table)
        slc_past = gather_pages(cache_slc_kv[l], page_table)
        past_len = cmp_past.shape[1]
        ys, c2, s2, w2, g2, v2 = trunk_layer(
            ys, past_len, cmp_past, slc_past, cache_win_kv[l], past_len - w_buf, w_buf,
            state_conv[l], state_gdn[l], SAMPLE_Q_BLOCK, *wts)
        cmp_p.append(c1); cmp_s.append(c2); slc_p.append(s1); slc_s.append(s2)
        win_p.append(w1); win_s.append(w2); gdn_p.append(g1); gdn_s.append(g2)
        conv_p.append(v1); conv_s.append(v2)
    y_prompt = rmsnorm(yp, norm_final)
    y_sample = rmsnorm(ys, norm_final)
    return (y_prompt, y_sample,
            jnp.stack(cmp_p), jnp.stack(cmp_s), jnp.stack(slc_p), jnp.stack(slc_s),
            jnp.stack(win_p), jnp.stack(win_s), jnp.stack(gdn_p), jnp.stack(gdn_s),
            jnp.stack(conv_p), jnp.stack(conv_s))
```

```python
import numpy as np
import ml_dtypes
import concourse.bass as bass
import concourse.mybir as mybir
from concourse.bass_utils import run_bass_kernel_spmd

F32 = mybir.dt.float32
BF16 = mybir.dt.bfloat16
I32 = mybir.dt.int32
AF = mybir.ActivationFunctionType
ALU = mybir.AluOpType
AX = mybir.AxisListType

D_MODEL = 1024
KC = 8
GDN_QKV = 1536
N_IN = 3360
D_FF = 2816
FC = 22
NEG = -30000.0
EPS = 1e-6

EPOCH = 20000
NDMASEM = 48
ENGS = ("pe", "act", "dve", "pool", "sp")


class FW:
    def __init__(self, nc):
        self.nc = nc
        self.eng = {"pe": nc.tensor, "act": nc.scalar, "dve": nc.vector,
                    "pool": nc.gpsimd, "sp": nc.sync}
        self.ops = {e: [] for e in ENGS}
        self.cnt = {e: 0 for e in ENGS}
        self.esems = {e: [] for e in ENGS}
        self.dsems = []
        self.dcnt = [0] * NDMASEM
        self.dnext = 0
        self.dnext_pool = 0
        self.waited = {e: {} for e in ENGS}
        self.lastw = {}
        self.readers = {}
        self._stack = []
        self.n_instr = 0
        self.ccsem = None
        self.cccnt = 0

    def _alloc_sem(self, name):
        cm = self.nc.semaphore(name)
        h = cm.__enter__()
        self._stack.append(cm)
        return h

    def _esem(self, e, epoch):
        while len(self.esems[e]) <= epoch:
            self.esems[e].append(self._alloc_sem(f"s_{e}_{len(self.esems[e])}"))
        return self.esems[e][epoch]

    def setup(self):
        for k in range(NDMASEM):
            self.dsems.append(self._alloc_sem(f"s_dma_{k}"))
        self.ccsem = self._alloc_sem("s_cc")

    def _need(self, e, ev):
        if ev is None:
            return
        key, h, val, src = ev
        if src == e and e == "pe":
            return
        if self.waited[e].get(key, 0) >= val:
            return
        self.waited[e][key] = val
        eng = self.eng[e]
        self.ops[e].append(lambda eng=eng, h=h, val=val: eng.wait_ge(h, val))

    @staticmethod
    def _psum_rw(reads, writes):
        extra = [k for k in reads if isinstance(k, tuple) and k and k[0] == "ps" and k not in writes]
        if extra:
            writes = list(writes) + extra
        return reads, writes

    def _deps(self, e, reads, writes):
        reads, writes = self._psum_rw(reads, writes)
        for k in reads:
            self._need(e, self.lastw.get(k))
        for k in writes:
            self._need(e, self.lastw.get(k))
            for ev in self.readers.get(k, ()):
                self._need(e, ev)

    def _commit(self, ev, reads, writes):
        reads, writes = self._psum_rw(reads, writes)
        for k in reads:
            lst = self.readers.setdefault(k, [])
            lst.append(ev)
            if len(lst) > 8:
                d = {}
                for x in lst:
                    kk = x[0]
                    if kk not in d or d[kk][2] < x[2]:
                        d[kk] = x
                self.readers[k] = list(d.values())
        for k in writes:
            self.lastw[k] = ev
            self.readers[k] = []

    def op(self, e, fn, reads=(), writes=()):
        self._deps(e, reads, writes)
        self.cnt[e] += 1
        c = self.cnt[e]
        epoch = (c - 1) // EPOCH
        h = self._esem(e, epoch)
        val = c - epoch * EPOCH
        ev = ((e, epoch), h, val, e)
        self.ops[e].append(lambda fn=fn, h=h: fn().then_inc(h, 1))
        self._commit(ev, reads, writes)
        self.n_instr += 1
        return ev

    def mm(self, fns, reads=(), writes=()):
        self._deps("pe", reads, writes)
        for fn in fns[:-1]:
            self.ops["pe"].append(fn)
        self.n_instr += len(fns) - 1
        return self.op("pe", fns[-1], reads, writes)

    def dma(self, q, fn, reads=(), writes=()):
        self._deps(q, reads, writes)
        half = NDMASEM // 2
        if q == "pool":
            k = half + self.dnext_pool
            self.dnext_pool = (self.dnext_pool + 1) % half
        else:
            k = self.dnext
            self.dnext = (self.dnext + 1) % half
        h = self.dsems[k]
        prev = self.dcnt[k]
        if prev:
            self._need(q, (("dma", k), h, prev, "dma"))
        self.dcnt[k] += 16
        val = self.dcnt[k]
        ev = (("dma", k), h, val, "dma")
        self.ops[q].append(lambda fn=fn, h=h: fn().then_inc(h, 16))
        self._commit(ev, reads, writes)
        self.n_instr += 1
        return ev

    def cc(self, fn, reads=(), writes=()):
        q = "pool"
        self._deps(q, reads, writes)
        self.cccnt += 1
        h = self.ccsem
        val = self.cccnt
        ev = (("cc", 0), h, val, "cc")
        self.ops[q].append(lambda fn=fn, h=h: fn().then_inc(h, 1))
        self._need(q, ev)
        self._commit(ev, reads, writes)
        return ev

    def barrier(self):
        evs = []
        for e in ENGS:
            c = self.cnt[e]
            if c:
                epoch = (c - 1) // EPOCH
                evs.append(((e, epoch), self.esems[e][epoch], c - epoch * EPOCH, e))
        for k in range(NDMASEM):
            if self.dcnt[k]:
                evs.append((("dma", k), self.dsems[k], self.dcnt[k], "dma"))
        for e in ENGS:
            for ev in evs:
                if ev[3] == e:
                    continue
                self._need(e, ev)

    def wait_all(self, e, keys):
        for k in keys:
            self._need(e, self.lastw.get(k))

    def emit(self):
        nc = self
        with self.nc.Block() as block:
            @block.tensor
            def _(x):
                for f in self.ops["pe"]:
                    f()

            @block.scalar
            def _(x):
                for f in self.ops["act"]:
                    f()

            @block.vector
            def _(x):
                for f in self.ops["dve"]:
                    f()

            @block.gpsimd
            def _(x):
                for f in self.ops["pool"]:
                    f()

            @block.sync
            def _(x):
                for f in self.ops["sp"]:
                    f()

    def close(self):
        while self._stack:
            self._stack.pop().__exit__(None, None, None)


class Cfg:
    def __init__(self, SEQ=8192, DEPTH=4, PAST=2048, stop_after=None):
        self.SEQ, self.DEPTH, self.PAST = SEQ, DEPTH, PAST
        self.NT = SEQ // 128
        self.NJ = self.NT // 4
        self.TP = self.NJ * 128
        self.TS = 128
        self.TL = self.TP + self.TS
        self.NPG = PAST // 128
        self.NPOOL = (5 * 128 * self.NPG + 3) // 4
        self.stop_after = stop_after


class Builder:
    def __init__(self, cfg):
        self.cfg = cfg
        self.nc = bass.Bass("TRN2", target_bir_lowering=False)
        self.fw = FW(self.nc)
        self._cms = []
        self.rr = 0
        self.psn = 0

    def sb(self, name, shape, dt):
        cm = self.nc.sbuf_tensor(name, shape, dt)
        t = cm.__enter__()
        self._cms.append(cm)
        return t

    def arena_setup(self, words):
        self.arena = self.sb("arena", [128, words], F32)
        self.awords = words
        self.abump = 0

    def areset(self):
        self.fw.barrier()
        self.abump = 0

    def aalloc(self, name, shape, dt):
        free = 1
        for d in shape[1:]:
            free *= d
        esz = 4 if dt in (F32, I32) else 2
        words = (free * esz + 3) // 4
        words = (words + 7) // 8 * 8
        off = self.abump
        self.abump += words
        assert self.abump <= self.awords, f"arena overflow allocating {name}: {self.abump} > {self.awords}"
        ap = self.arena[:, off:off + words]
        if dt != F32:
            ap = ap.bitcast(dt)
        ap = ap[0:shape[0], 0:free]
        if len(shape) == 3:
            ap = ap.rearrange("p (a b) -> p a b", a=shape[1])
        elif len(shape) == 4:
            ap = ap.rearrange("p (a b c) -> p a b c", a=shape[1], b=shape[2])
        return ap

    def dram(self, name, shape, dt, kind):
        return self.nc.dram_tensor(name, list(shape), dt, kind=kind)

    def psum_setup(self):
        self.ps = []
        for i in range(8):
            cm = self.nc.psum_tensor(f"ps{i}", [128, 512], F32)
            self.ps.append(cm.__enter__())
            self._cms.append(cm)

    def psb(self):
        lim = getattr(self, "ps_lim", 8)
        i = self.psn % lim
        self.psn = (i + 1) % lim
        return self.ps[i], ("ps", i)

    def ev_eng(self):
        self.rr += 1
        return "act" if self.rr % 2 else "dve"

    def copy(self, e, out, in_, reads, writes):
        nc = self.nc
        if e == "act":
            return self.fw.op("act", lambda: nc.scalar.activation(out=out, in_=in_, func=AF.Copy), reads, writes)
        if e == "dve":
            return self.fw.op("dve", lambda: nc.vector.tensor_copy(out=out, in_=in_), reads, writes)
        return self.fw.op("pool", lambda: nc.gpsimd.tensor_copy(out=out, in_=in_), reads, writes)

    def build(self):
        cfg, nc, fw = self.cfg, self.nc, self.fw
        L = cfg.DEPTH
        TL, TP, TS = cfg.TL, cfg.TP, cfg.TS
        self.xT_in = self.dram("xT", [D_MODEL, TL], F32, "ExternalInput")
        self.w_in = self.dram("w_in", [L, D_MODEL, N_IN], F32, "ExternalInput")
        self.w_out = self.dram("w_out", [L, D_MODEL, D_MODEL], F32, "ExternalInput")
        self.w_f1 = self.dram("w_ffn_in", [L, D_MODEL, 2 * D_FF], F32, "ExternalInput")
        self.w_f2 = self.dram("w_ffn_out", [L, D_FF, D_MODEL], F32, "ExternalInput")
        self.gvec = self.dram("gvec", [128, L * 2 * KC + KC], F32, "ExternalInput")
        self.yT = self.dram("yT", [D_MODEL, TL], F32, "ExternalOutput")
        self.kvT = self.dram("kvT", [L, 768, TL], F32, "ExternalOutput")
        self.convT = self.dram("convT", [L, GDN_QKV, cfg.NJ, 3], F32, "ExternalOutput")
        self.wb_in = self.dram("wb_in", [L, 128, KC, N_IN], BF16, "Internal")
        self.wb_out = self.dram("wb_out", [L, 128, KC, D_MODEL], BF16, "Internal")
        self.wb_f1 = self.dram("wb_f1", [L, 128, KC, 2 * D_FF], BF16, "Internal")
        self.wb_f2 = self.dram("wb_f2", [L, 128, FC, D_MODEL], BF16, "Internal")

        NJ = cfg.NJ
        self.GXR = 1544
        self.cmask_in = self.dram("cmask", [128, 8, 128], F32, "ExternalInput")
        self.seqind_in = self.dram("seqind", [128, 16], F32, "ExternalInput")
        self.mmask_in = self.dram("mmask", [128, 4, 128], F32, "ExternalInput")
        self.sel_in = self.dram("selrows", [8, 8, 128], F32, "ExternalInput")
        self.cwH_in = self.dram("cwH", [128, L, 3, 4], F32, "ExternalInput")
        self.cwS_in = self.dram("cwS", [128, L, 12, 4], F32, "ExternalInput")
        self.gba_in = self.dram("gba", [8, L, 2], F32, "ExternalInput")
        self.gnv_in = self.dram("gnv", [128, L], F32, "ExternalInput")
        self.idx_in = self.dram("idxtab", [128, 8], I32, "ExternalInput")
        self.gdnP = self.dram("gdnP", [L, 128, 128], F32, "ExternalOutput")

        self.gx = self.dram("gx", [NJ, self.GXR, 128], F32, "Internal")
        self.gxg = self.dram("gxg", [NJ, 4 * self.GXR, 128], F32, "Internal")
        self.OG = min(4, NJ)
        self.NOG = NJ // self.OG
        self.ox = self.dram("ox", [self.NOG, 512, self.OG * 128], F32, "Internal")
        self.oxg = self.dram("oxg", [self.NOG, 4 * 512, self.OG * 128], F32, "Internal")
        NT = cfg.NT
        self.KG = min(8, NJ)
        self.NKG = NJ // self.KG
        self.VGp = min(16, NJ)
        self.NVG = NJ // self.VGp
        self.qx = self.dram("qx", [8, 64, TL], BF16, "Internal")
        self.kx = self.dram("kx", [self.NKG, self.KG * 512, 128], BF16, "Internal")
        self.kxg = self.dram("kxg", [self.NKG, 4 * self.KG * 512, 128], BF16, "Internal")
        self.vx = self.dram("vx", [self.NVG, self.VGp * 128, 256], BF16, "Internal")
        self.vxg = self.dram("vxg", [self.NVG, 4 * self.VGp * 128, 256], BF16, "Internal")
        self.kxs = self.dram("kxs", [512, 128], BF16, "Internal")
        self.vxs = self.dram("vxs", [128, 256], BF16, "Internal")
        self.gt = self.dram("gt", [TL, 24], F32, "Internal")
        self.G_in = self.dram("Gtab", [128, 128 * max(NT, 17)], BF16, "ExternalInput")
        self.kaug_in = self.dram("kaug", [4, NT * 128], BF16, "ExternalInput")
        self.caug_in = self.dram("caug", [4, 512], BF16, "ExternalInput")
        self.qaug_in = self.dram("qaug", [NJ, 4, 8, 128], BF16, "ExternalInput")
        self.cmk_in = self.dram("cmk", [NJ, 128, 512], BF16, "ExternalInput")
        self.fb_in = self.dram("fbt", [NJ, 128, 128], F32, "ExternalInput")
        self.cm_in = self.dram("cmt", [128, 4, 128], BF16, "ExternalInput")
        self.wm_in = self.dram("wmt", [128, 2, 8, 128], BF16, "ExternalInput")
        self.cw_in = self.dram("cmpw", [64, L, 64, 64], F32, "ExternalInput")
        self.pe_in = self.dram("cmppe", [64, L, 64], F32, "ExternalInput")
        self.ng_in = self.dram("nsag", [128, L, 64], F32, "ExternalInput")
        NPC = cfg.NPG
        self.sgdn_in = self.dram("sgdn", [L, 16, 4, 128, 128], F32, "ExternalInput")
        self.sconv_in = self.dram("sconv", [L, 12, 128, 16, 3], F32, "ExternalInput")
        self.gdnS = self.dram("gdnS", [L, 16, 4, 128, 128], F32, "ExternalOutput")
        self.ptab_in = self.dram("ptab", [1, 16 * NPC], I32, "ExternalInput")
        self.iota_in = self.dram("iotac", [128, 1], F32, "ExternalInput")
        self.ccmp_in = self.dram("cache_cmp", [L * cfg.NPOOL * 128, 256], F32, "ExternalInput")
        self.cslc_in = self.dram("cache_slc", [L * cfg.NPOOL * 128, 256], F32, "ExternalInput")
        self.cwin_in = self.dram("cache_win", [L, 16, 512, 256], F32, "ExternalInput")
        self.winS = self.dram("winS", [L, 16, 504, 256], F32, "ExternalOutput")
        self.qaugs_in = self.dram("qaugs", [4, 8, 8], BF16, "ExternalInput")
        self.cmks_in = self.dram("cmks", [8, 128], BF16, "ExternalInput")
        self.fbs_in = self.dram("fbs", [8, 64], F32, "ExternalInput")
        self.cms_in = self.dram("cms", [128, 8], BF16, "ExternalInput")
        self.wms_in = self.dram("wms", [128, 5, 8], BF16, "ExternalInput")
        fw.setup()
        self.psum_setup()
        self.xT = self.sb("xT_sb", [128, KC, TL], F32)
        self.hT = self.sb("hT_sb", [128, KC, TL], BF16)
        self.arena_setup(22000)
        self.gv = self.sb("gv_sb", [128, L * 2 * KC + KC], F32)
        self.ones_bf = self.sb("ones_bf", [128, 128], BF16)
        self.o_d = self.dram("o_d", [KC, 128, TL], BF16, "Internal")
        self.gate_d = self.dram("gate_d", [4, 128, TL], BF16, "Internal")

        fw.op("pool", lambda: nc.gpsimd.memset(self.ones_bf[:], 1.0), writes=["ones_bf"])
        self.cmask = self.sb("cmask_sb", [128, 8, 128], F32)
        self.seqind = self.sb("seqind_sb", [128, 16], F32)
        self.selr = self.sb("selr_sb", [8, 8, 128], F32)
        self.cwH = self.sb("cwH_sb", [128, L, 3, 4], F32)
        self.cwS = self.sb("cwS_sb", [128, L, 12, 4], F32)
        self.gba = self.sb("gba_sb", [8, L, 2], F32)
        self.negA = self.sb("negA_sb", [8, L], F32)
        self.gnv = self.sb("gnv_sb", [128, L], F32)
        self.idxt = self.sb("idx_sb", [128, 8], I32)
        self.ones_f = self.sb("ones_f", [128, 128], F32)
        self.onec = self.sb("onec", [128, 1], F32)
        for (t, src, key) in [(self.cmask, self.cmask_in, "cmask"), (self.seqind, self.seqind_in, "seqind"), (self.selr, self.sel_in, "selr"),
                              (self.cwH, self.cwH_in, "cwH"), (self.cwS, self.cwS_in, "cwS"), (self.gba, self.gba_in, "gba"),
                              (self.gnv, self.gnv_in, "gnv"), (self.idxt, self.idx_in, "idxt")]:
            fw.dma("sp", lambda t=t, src=src: nc.sync.dma_start(out=t[:], in_=src.ap()), writes=[key])
        fw.op("pool", lambda: nc.gpsimd.memset(self.ones_f[:], 1.0), writes=["ones_f"])
        fw.op("pool", lambda: nc.gpsimd.memset(self.onec[:], 1.0), writes=["onec"])
        for l in range(L):
            fw.op("act", lambda l=l: nc.scalar.activation(out=self.negA[:, l:l + 1], in_=self.gba[:, l, 1:2], func=AF.Exp), reads=["gba"], writes=["negA"])
        fw.op("dve", lambda: nc.vector.tensor_scalar(out=self.negA[:], in0=self.negA[:], scalar1=-1.0, scalar2=None, op0=ALU.mult), reads=["negA"], writes=["negA"])
        self.epsc = self.sb("epsc", [128, 1], F32)
        fw.op("pool", lambda: nc.gpsimd.memset(self.epsc[:], EPS), writes=["epsc"])
        fw.dma("sp", lambda: nc.sync.dma_start(out=self.gv[:], in_=self.gvec.ap()), writes=["gv"])
        for k in range(KC):
            fw.dma("sp", lambda k=k: nc.sync.dma_start(out=self.xT[:, k, :], in_=self.xT_in.ap()[k * 128:(k + 1) * 128, :]),
                   writes=[("xT", k)])
        self.convert_weights()
        for l in range(L):
            self.phase_a(l)
            import os as _os
            if not _os.environ.get("K_SKIP_GDN"):
                self.gdn_prompt(l)
            if not _os.environ.get("K_SKIP_NSA"):
                self.nsa_prompt(l)
            if not _os.environ.get("K_SKIP_GS"):
                self.gdn_sample(l)
            if not _os.environ.get("K_SKIP_NS"):
                self.nsa_sample(l)
            self.phase_c(l)
        self.final_norm()
        fw.barrier()
        fw.emit()
        fw.close()
        for cm in reversed(self._cms):
            cm.__exit__(None, None, None)
        return nc

    def convert_weights(self):
        cfg, nc, fw = self.cfg, self.nc, self.fw
        L = cfg.DEPTH
        W = 2816
        stg = [self.aalloc(f"cv_f{i}", [128, W], F32) for i in range(3)]
        stb = [self.aalloc(f"cv_b{i}", [128, W], BF16) for i in range(3)]
        engs = ["act", "dve", "pool"]
        n = 0
        jobs = []
        for l in range(L):
            for k in range(KC):
                jobs.append((self.w_in.ap()[l, k * 128:(k + 1) * 128, 0:1680], self.wb_in.ap()[l, :, k, 0:1680], 1680))
                jobs.append((self.w_in.ap()[l, k * 128:(k + 1) * 128, 1680:3360], self.wb_in.ap()[l, :, k, 1680:3360], 1680))
                jobs.append((self.w_out.ap()[l, k * 128:(k + 1) * 128, :], self.wb_out.ap()[l, :, k, :], 1024))
                jobs.append((self.w_f1.ap()[l, k * 128:(k + 1) * 128, 0:2816], self.wb_f1.ap()[l, :, k, 0:2816], 2816))
                jobs.append((self.w_f1.ap()[l, k * 128:(k + 1) * 128, 2816:5632], self.wb_f1.ap()[l, :, k, 2816:5632], 2816))
            for k in range(FC):
                jobs.append((self.w_f2.ap()[l, k * 128:(k + 1) * 128, :], self.wb_f2.ap()[l, :, k, :], 1024))
        for src, dst, w in jobs:
            i = n % 3
            n += 1
            q = "sp" if n % 2 else "pool"
            fw.dma("sp", lambda src=src, i=i, w=w: nc.sync.dma_start(out=stg[i][:, 0:w], in_=src), writes=[("cvf", i)])
            self.copy(engs[i], stb[i][:, 0:w], stg[i][:, 0:w], reads=[("cvf", i)], writes=[("cvb", i)])
            fw.dma("sp", lambda dst=dst, i=i, w=w: nc.sync.dma_start(out=dst, in_=stb[i][:, 0:w]), reads=[("cvb", i)], writes=["wb"])
        self.areset()

    def tok_blocks(self):
        cfg = self.cfg
        bl = [(c0, min(512, cfg.TP - c0)) for c0 in range(0, cfg.TP, 512)]
        bl.append((cfg.TP, cfg.TS))
        return bl

    def rmsnorm_to(self, gcol0, out_tile, out_key, out_dt_bf=True):
        cfg, nc, fw = self.cfg, self.nc, self.fw
        TL = cfg.TL
        sq = self.sq
        blocks = self.tok_blocks()
        for (c0, n) in blocks:
            for k in range(KC):
                fw.op("act", lambda k=k, c0=c0, n=n: nc.scalar.activation(out=sq[:, k, 0:n], in_=self.xT[:, k, c0:c0 + n], func=AF.Square),
                      reads=[("xT", k)], writes=[("sq", k)])
            pt, pk = self.psb()
            fw.mm([lambda k=k, n=n, pt=pt: nc.tensor.matmul(pt[:, 0:n], lhsT=self.ones_bf[:], rhs=sq[:, k, 0:n], start=(k == 0), stop=(k == KC - 1))
                   for k in range(KC)], reads=[("sq", k) for k in range(KC)] + ["ones_bf"], writes=[pk])
            rs = self.rstd
            fw.op("act", lambda n=n, pt=pt: nc.scalar.activation(out=rs[:, 0:n], in_=pt[:, 0:n], func=AF.Ln, bias=self.epsc[:, 0:1], scale=1.0 / D_MODEL),
                  reads=[pk, "epsc"], writes=["rstd"])
            fw.op("act", lambda n=n: nc.scalar.activation(out=rs[:, 0:n], in_=rs[:, 0:n], func=AF.Exp, scale=-0.5),
                  reads=["rstd"], writes=["rstd"])
            for k in range(KC):
                fw.op("dve", lambda k=k, c0=c0, n=n: nc.vector.scalar_tensor_tensor(
                    out=out_tile[:, k, c0:c0 + n], in0=self.xT[:, k, c0:c0 + n], scalar=self.gv[:, gcol0 + k:gcol0 + k + 1],
                    in1=rs[:, 0:n], op0=ALU.mult, op1=ALU.mult),
                    reads=[("xT", k), "rstd", "gv"], writes=[(out_key, k)])

    def phase_a(self, l):
        cfg, nc, fw = self.cfg, self.nc, self.fw
        TL, TP, TS = cfg.TL, cfg.TP, cfg.TS
        self.areset()
        self.sq = self.aalloc("sq_sb", [128, KC, 512], BF16)
        self.rstd = self.aalloc("rstd_sb", [128, 512], F32)
        self.wt = [self.aalloc(f"wt{i}", [128, KC, 512], BF16) for i in range(2)]
        self.stg = [self.aalloc(f"stg{i}", [128, 512], F32) for i in range(4)]
        self.wtn = 0
        self.stn = 0
        self.sg8 = self.aalloc("sg8", [8, 512], F32)
        self.gg8 = self.aalloc("gg8", [8, 512], F32)
        self.gstg = [self.aalloc(f"gstg{i}", [128, 512], BF16) for i in range(2)]
        self.gtst = [self.aalloc(f"gtst{i}", [128, 24], F32) for i in range(2)]
        self.gsn = 0
        if l == 0:
            self.sg8s = self.sb("sg8s", [8, 128], F32)
            self.gg8s = self.sb("gg8s", [8, 128], F32)
            self.sqkv = self.sb("sqkv", [128, 12, 128], F32)
        self.rmsnorm_to(l * 2 * KC, self.hT, "hT")
        chunks = []
        for h in range(4):
            chunks.append((128 * h, 128, "gq", h))
        for h in range(4):
            chunks.append((512 + 128 * h, 128, "gk", h))
        for h in range(4):
            chunks.append((1024 + 128 * h, 128, "gv", h))
        chunks.append((1536, 8, "ba", 0))
        for h in range(4):
            chunks.append((1544 + 128 * h, 128, "gg", h))
        for hh in range(8):
            chunks.append((2056 + 64 * hh, 64, "nq", hh))
        for br in range(3):
            b0 = 2568 + 256 * br
            chunks.append((b0, 64, "k", (br, 0)))
            chunks.append((b0 + 64, 64, "k", (br, 1)))
            chunks.append((b0 + 128, 128, "v", br))
        chunks.append((3336, 24, "ng", 0))
        groups = []
        cur = []
        for ch in chunks:
            if cur and (ch[0] + ch[1] - cur[0][0] > 512 or ch[0] != cur[-1][0] + cur[-1][1]):
                groups.append(cur)
                cur = []
            cur.append(ch)
        groups.append(cur)
        blocks = self.tok_blocks()
        for grp in groups:
            g0 = grp[0][0]
            gw = grp[-1][0] + grp[-1][1] - g0
            wi = self.wtn % 2
            self.wtn += 1
            wt = self.wt[wi]
            fw.dma("sp", lambda g0=g0, gw=gw, wt=wt: nc.sync.dma_start(out=wt[:, :, 0:gw], in_=self.wb_in.ap()[l, :, :, g0:g0 + gw]),
                   reads=["wb"], writes=[("wt", wi)])
            for (f0, M, kind, idx) in grp:
                for (c0, n) in blocks:
                    pt, pk = self.psb()
                    fw.mm([lambda k=k, pt=pt, M=M, n=n, f0=f0, c0=c0, wt=wt, g0=g0: nc.tensor.matmul(
                        pt[0:M, 0:n], lhsT=wt[:, k, f0 - g0:f0 - g0 + M], rhs=self.hT[:, k, c0:c0 + n], start=(k == 0), stop=(k == KC - 1))
                        for k in range(KC)], reads=[("wt", wi)] + [("hT", k) for k in range(KC)], writes=[pk])
                    self.proj_out(l, kind, idx, f0, M, c0, n, pt, pk)

    def proj_out(self, l, kind, idx, f0, M, c0, n, pt, pk):
        cfg, nc, fw = self.cfg, self.nc, self.fw
        si = self.stn % 4
        self.stn += 1
        st = self.stg[si]
        e = self.ev_eng()
        self.copy(e, st[0:M, 0:n], pt[0:M, 0:n], reads=[pk], writes=[("stg", si)])
        if kind == "nq":
            qi = self.gsn % 2
            self.gsn += 1
            qs = self.gstg[qi]
            fw.op("act", lambda: nc.scalar.activation(out=qs[0:64, 0:n], in_=st[0:64, 0:n], func=AF.Copy, scale=0.125), reads=[("stg", si)], writes=[("gstg", qi)])
            fw.dma("sp", lambda: nc.sync.dma_start(out=self.qx.ap()[idx, :, c0:c0 + n], in_=qs[0:64, 0:n]), reads=[("gstg", qi)], writes=["qx"])
        if kind == "ng":
            nt = n // 128
            for tt in range(nt):
                pt2, pk2 = self.psb()
                fw.mm([lambda tt=tt, pt2=pt2: nc.tensor.transpose(pt2[:, 0:24], st[0:24, tt * 128:(tt + 1) * 128], self.cmask[0:24, 3, 0:24])], reads=[("stg", si), "cmask"], writes=[pk2])
                gi = self.gsn % 2
                self.gsn += 1
                gsf = self.gtst[gi]
                fw.op("act", lambda pt2=pt2, gsf=gsf: nc.scalar.activation(out=gsf[:, 0:24], in_=pt2[:, 0:24], func=AF.Sigmoid), reads=[pk2], writes=[("gtst", gi)])
                fw.dma("sp", lambda tt=tt, gsf=gsf: nc.sync.dma_start(out=self.gt.ap()[c0 + tt * 128:c0 + (tt + 1) * 128, :], in_=gsf[:, 0:24]), reads=[("gtst", gi)], writes=["gt"])
        if kind == "k" or (kind == "v" and idx == 0):
            br_ = idx[0] if kind == "k" else 0
            krow = ({0: 0, 1: 256, 2: 384}[br_] + 64 * idx[1]) if kind == "k" else 128
            qi = self.gsn % 2
            self.gsn += 1
            qs = self.gstg[qi]
            self.copy("pool", qs[0:M, 0:n], st[0:M, 0:n], reads=[("stg", si)], writes=[("gstg", qi)])
            if c0 < cfg.TP:
                j0, nj = c0 // 128, n // 128
                kg0 = j0 // self.KG
                jj0 = j0 % self.KG
                kdst = self.kx.ap()[kg0].rearrange("(j f) c -> j f c", f=512)[jj0:jj0 + nj, krow:krow + M, :].rearrange("j f c -> f j c")
                fw.dma("sp", lambda: nc.sync.dma_start(out=kdst, in_=qs[0:M, 0:n].rearrange("p (j c) -> p j c", c=128)), reads=[("gstg", qi)], writes=["kx"])
            else:
                fw.dma("sp", lambda: nc.sync.dma_start(out=self.kxs.ap()[krow:krow + M, :], in_=qs[0:M, 0:n]), reads=[("gstg", qi)], writes=["kxs"])
        if kind == "v" and idx in (1, 2):
            nt = n // 128
            pt2, pk2 = self.psb()
            for tt in range(nt):
                fw.mm([lambda tt=tt, pt2=pt2: nc.tensor.transpose(pt2[:, tt * 128:(tt + 1) * 128], st[0:128, tt * 128:(tt + 1) * 128], self.cmask[:, 3, :])], reads=[("stg", si), "cmask"], writes=[pk2])
            qi = self.gsn % 2
            self.gsn += 1
            qs = self.gstg[qi]
            self.copy("dve", qs[:, 0:n], pt2[:, 0:n], reads=[pk2], writes=[("gstg", qi)])
            vb0 = 128 * (idx - 1)
            if c0 < cfg.TP:
                j0 = c0 // 128
                vg0, jj0 = j0 // self.VGp, j0 % self.VGp
                vdst = self.vx.ap()[vg0].rearrange("(j t) f -> j t f", t=128)[jj0:jj0 + nt, :, vb0:vb0 + 128].rearrange("j t f -> t j f")
                fw.dma("sp", lambda: nc.sync.dma_start(out=vdst, in_=qs[:, 0:n].rearrange("p (j f) -> p j f", f=128)), reads=[("gstg", qi)], writes=["vx"])
            else:
                fw.dma("sp", lambda: nc.sync.dma_start(out=self.vxs.ap()[:, vb0:vb0 + 128], in_=qs[:, 0:128]), reads=[("gstg", qi)], writes=["vxs"])
        if kind in ("k", "v"):
            br = idx[0] if kind == "k" else idx
            row = 256 * br + (64 * idx[1] if kind == "k" else 128)
            fw.dma("sp", lambda: nc.sync.dma_start(out=self.kvT.ap()[l, row:row + M, c0:c0 + n], in_=st[0:M, 0:n]),
                   reads=[("stg", si)], writes=["kvT_out"])
        if kind in ("gq", "gk", "gv") and c0 < cfg.TP:
            row0 = {"gq": 0, "gk": 512, "gv": 1024}[kind] + 128 * idx
            j0, nj = c0 // 128, n // 128
            dst = self.gx.ap()[j0:j0 + nj, row0:row0 + M, :].rearrange("j f c -> f j c")
            fw.dma("pool", lambda: nc.gpsimd.dma_start(out=dst, in_=st[0:M, 0:n].rearrange("p (j c) -> p j c", c=128)),
                   reads=[("stg", si)], writes=["gx"])
        if kind in ("gq", "gk", "gv") and c0 >= cfg.TP:
            ci = {"gq": 0, "gk": 4, "gv": 8}[kind] + idx
            self.copy("pool", self.sqkv[:, ci, :], st[0:M, 0:n], reads=[("stg", si)], writes=[("sqkv", ci)])
        if kind == "gg":
            gi = self.gsn % 2
            self.gsn += 1
            gs = self.gstg[gi]
            fw.op("act", lambda: nc.scalar.activation(out=gs[:, 0:n], in_=st[0:M, 0:n], func=AF.Silu),
                  reads=[("stg", si)], writes=[("gstg", gi)])
            fw.dma("sp", lambda: nc.sync.dma_start(out=self.gate_d.ap()[idx, :, c0:c0 + n], in_=gs[:, 0:n]), reads=[("gstg", gi)], writes=["gate_d"])
        if kind == "ba":
            prompt = c0 < cfg.TP
            sg8 = self.sg8 if prompt else self.sg8s
            gg8 = self.gg8 if prompt else self.gg8s
            kk = "p" if prompt else "s"
            fw.op("act", lambda: nc.scalar.activation(out=sg8[:, 0:n], in_=st[0:8, 0:n], func=AF.Sigmoid), reads=[("stg", si)], writes=[("sg8", kk)])
            fw.op("act", lambda: nc.scalar.activation(out=gg8[:, 0:n], in_=st[0:8, 0:n], func=AF.Exp, bias=self.gba[:, l, 0:1]), reads=[("stg", si), "gba"], writes=[("gg8", kk)])
            fw.op("act", lambda: nc.scalar.activation(out=gg8[:, 0:n], in_=gg8[:, 0:n], func=AF.Ln, bias=self.onec[0:8, 0:1]), reads=[("gg8", kk), "onec"], writes=[("gg8", kk)])
            fw.op("dve", lambda: nc.vector.tensor_scalar(out=gg8[:, 0:n], in0=gg8[:, 0:n], scalar1=self.negA[:, l:l + 1], scalar2=None, op0=ALU.mult), reads=[("gg8", kk), "negA"], writes=[("gg8", kk)])
            if prompt:
                j0, nj = c0 // 128, n // 128
                bgv = self.gx.ap()[j0:j0 + nj, 1536:1544, :].rearrange("j (h two) c -> two h j c", two=2)
                fw.dma("pool", lambda: nc.gpsimd.dma_start(out=bgv[0], in_=sg8[0:4, 0:n].rearrange("p (j c) -> p j c", c=128)), reads=[("sg8", kk)], writes=["gx"])
                fw.dma("pool", lambda: nc.gpsimd.dma_start(out=bgv[1], in_=gg8[4:8, 0:n].rearrange("p (j c) -> p j c", c=128)), reads=[("gg8", kk)], writes=["gx"])
        if kind in ("gq", "gk", "gv"):
            base = {"gq": 0, "gk": 512, "gv": 1024}[kind] + 128 * idx
            t0 = c0 // 128
            nt = n // 128
            src = st[0:M, 0:n].rearrange("p (t c) -> p t c", c=128)[:, :, 125:128]
            if c0 < cfg.TP:
                fw.dma("sp", lambda: nc.sync.dma_start(out=self.convT.ap()[l, base:base + M, t0:t0 + nt, :], in_=src),
                       reads=[("stg", si)], writes=["conv_out"])
            else:
                src2 = st[0:M, 0:n].rearrange("p (s c) -> p s c", c=8)[:, :, 5:8]
                fw.dma("sp", lambda: nc.sync.dma_start(out=self.convS.ap()[l, base:base + M, :, :], in_=src2),
                       reads=[("stg", si)], writes=["conv_out"])

    def mixers_stub(self, l):
        nc, fw = self.nc, self.fw
        for k in range(KC):
            fw.op("pool", lambda k=k: nc.gpsimd.memset(self.oT[:, k, :], 0.0), writes=[("hT", k)])


    def gdn_alloc(self):
        self.areset()
        N = 512
        sb = self.aalloc
        self.g_xe = [[sb(f"g_xe{i}_{c}", [128, 3 + N], F32) for c in range(3)] for i in range(2)]
        self.g_bg2 = [sb(f"g_bg2_{i}", [2, N], F32) for i in range(2)]
        self.g_y = [sb(f"g_y{c}", [128, N], F32) for c in range(3)]
        self.g_sq = sb("g_sq", [128, N], F32)
        self.g_rinv = sb("g_rinv", [128, N], F32)
        self.g_qn = sb("g_qn", [128, N], F32)
        self.g_kn = sb("g_kn", [128, N], F32)
        self.g_bgc = sb("g_bgc", [128, 4, 2], F32)
        self.g_Bbc = sb("g_Bbc", [128, N], F32)
        self.g_Gb = sb("g_Gb", [128, N], F32)
        self.g_dc = sb("g_dc", [128, 4], F32)
        self.g_ndc = sb("g_ndc", [128, 4], F32)
        self.g_dl = sb("g_dl", [128, 4], F32)
        self.g_edc = sb("g_edc", [128, 4], F32)
        self.g_ekd = sb("g_ekd", [128, 4], F32)
        self.g_gl = sb("g_gl", [128, 4], F32)
        self.g_tmpD = sb("g_tmpD", [128, N], F32)
        self.g_expo = sb("g_expo", [128, N], F32)
        self.g_Ebc = sb("g_Ebc", [128, N], F32)
        self.g_X = sb("g_X", [128, N], F32)
        self.g_P = [sb(f"g_P{i}", [128, N], F32) for i in range(2)]
        self.g_R = [sb(f"g_R{i}", [128, N], F32) for i in range(2)]
        self.g_M = sb("g_M", [128, N], F32)
        self.g_attnT = sb("g_attnT", [128, N], F32)
        self.g_Kbd = sb("g_Kbd", [128, N], F32)
        self.g_kd = sb("g_kd", [128, N], F32)
        self.g_Vb = sb("g_Vb", [128, N], F32)
        self.g_u = sb("g_u", [128, N], F32)
        self.g_wT = sb("g_wT", [128, N], F32)
        self.g_qdT = sb("g_qdT", [128, N], F32)
        self.g_vnew = sb("g_vnew", [128, 128], F32)
        self.g_S = sb("g_S", [128, 128], F32)
        self.g_ss = sb("g_ss", [128, 4], F32)
        self.g_on = sb("g_on", [128, 128], F32)
        self.g_junk = sb("g_junk", [128, 128], F32)
        self.g_oT = sb("g_oT", [128, N], F32)
        self.g_mm = sb("g_mm", [128, 4, 128], F32)
        self.fw.dma("sp", lambda: self.nc.sync.dma_start(out=self.g_mm, in_=self.mmask_in.ap()), writes=["g_mm"])
        self.g_ostg = sb("g_ostg", [128, 512], F32)
        self.g_gate = sb("g_gate", [128, 512], BF16)

    def gdn_chunk_math(self, NB, nlev, mU, mNEG, mSTR, beta_src, beta_row, g_src, g_row, nrows, ys, tag):
        nc, fw = self.nc, self.fw
        N = NB * 128
        IDENT = self.cmask[:, 3, :]
        U = self.cmask[:, mU, :]
        NEGM = self.cmask[:, mNEG, :]
        STR = self.cmask[:, mSTR, :]
        for c in range(3):
            fw.op("act", lambda c=c: nc.scalar.activation(out=ys[c][:, 0:N], in_=ys[c][:, 0:N], func=AF.Silu), reads=[("g_y", c)], writes=[("g_y", c)])
        for c, dst, sc in ((0, self.g_qn, 128 ** -0.5), (1, self.g_kn, 1.0)):
            fw.op("act", lambda c=c: nc.scalar.activation(out=self.g_sq[:, 0:N], in_=ys[c][:, 0:N], func=AF.Square), reads=[("g_y", c)], writes=["g_sq"])
            pt, pk = self.psb()
            fw.mm([lambda pt=pt: nc.tensor.matmul(pt[:, 0:N], lhsT=self.ones_f[:], rhs=self.g_sq[:, 0:N], start=True, stop=True)], reads=["g_sq", "ones_f"], writes=[pk])
            fw.op("act", lambda pt=pt: nc.scalar.activation(out=self.g_rinv[:, 0:N], in_=pt[:, 0:N], func=AF.Ln, bias=self.epsc[:, 0:1]), reads=[pk, "epsc"], writes=["g_rinv"])
            fw.op("act", lambda: nc.scalar.activation(out=self.g_rinv[:, 0:N], in_=self.g_rinv[:, 0:N], func=AF.Exp, scale=-0.5), reads=["g_rinv"], writes=["g_rinv"])
            fw.op("dve", lambda c=c, dst=dst, sc=sc: nc.vector.scalar_tensor_tensor(out=dst[:, 0:N], in0=ys[c][:, 0:N], scalar=sc, in1=self.g_rinv[:, 0:N], op0=ALU.mult, op1=ALU.mult),
                  reads=[("g_y", c), "g_rinv"], writes=[("g_qk", c)])
        import os as _os
        cm = float(_os.environ.get("K_CM", "99"))
        if cm <= 1:
            return None, None, None, None
        pB, pBk = self.psb()
        for t in range(NB):
            cs = slice(t * 128, (t + 1) * 128)
            pt, pk = self.psb()
            fw.mm([lambda pt=pt, cs=cs: nc.tensor.matmul(pt[:, 0:1], lhsT=beta_src[0:nrows, cs], rhs=self.cmask[0:nrows, 3, beta_row:beta_row + 1], start=True, stop=True),
                   lambda pt=pt, cs=cs: nc.tensor.matmul(pt[:, 1:2], lhsT=g_src[0:nrows, cs], rhs=self.cmask[0:nrows, 3, g_row:g_row + 1], start=True, stop=True)],
                  reads=[("bsrc", tag), ("gsrc", tag), "cmask"], writes=[pk])
            self.copy("dve", self.g_bgc[:, t, :], pt[:, 0:2], reads=[pk], writes=[("g_bgc", t)])
            fw.mm([lambda cs=cs, pB=pB: nc.tensor.matmul(pB[:, cs], lhsT=self.selr[0:nrows, beta_row, :], rhs=beta_src[0:nrows, cs], start=True, stop=True)],
                  reads=[("bsrc", tag), "selr"], writes=[pBk])
        self.copy("act", self.g_Bbc[:, 0:N], pB[:, 0:N], reads=[pBk], writes=["g_Bbc"])
        if cm <= 2:
            return None, None, None, None
        pD, pDk = self.psb()
        pc, pck = self.psb()
        for t in range(NB):
            cs = slice(t * 128, (t + 1) * 128)
            fw.op("dve", lambda t=t, cs=cs: nc.vector.tensor_copy(out=self.g_Gb[:, cs], in_=self.g_bgc[:, t, 1:2].to_broadcast([128, 128])), reads=[("g_bgc", t)], writes=[("g_Gb", t)])
            fw.mm([lambda cs=cs: nc.tensor.matmul(pD[:, cs], lhsT=self.g_Gb[:, cs], rhs=U, start=True, stop=True)], reads=[("g_Gb", t), "cmask"], writes=[pDk])
            fw.mm([lambda t=t: nc.tensor.matmul(pc[:, t:t + 1], lhsT=U, rhs=self.g_bgc[:, t, 1:2], start=True, stop=True)], reads=[("g_bgc", t), "cmask"], writes=[pck])
        if cm <= 2.2:
            return None, None, None, None
        self.copy("dve", self.g_dc[:, 0:NB], pc[:, 0:NB], reads=[pck], writes=["g_dc"])
        fw.op("dve", lambda: nc.vector.tensor_scalar(out=self.g_ndc[:, 0:NB], in0=self.g_dc[:, 0:NB], scalar1=-1.0, scalar2=None, op0=ALU.mult), reads=["g_dc"], writes=["g_ndc"])
        fw.op("act", lambda: nc.scalar.activation(out=self.g_edc[:, 0:NB], in_=self.g_dc[:, 0:NB], func=AF.Exp), reads=["g_dc"], writes=["g_edc"])
        fw.op("act", lambda: nc.scalar.activation(out=self.g_Ebc[:, 0:N], in_=pD[:, 0:N], func=AF.Exp), reads=[pDk], writes=["g_Ebc"])
        if cm <= 2.4:
            return None, None, None, None
        for t in range(NB):
            cs = slice(t * 128, (t + 1) * 128)
            fw.op("dve", lambda cs=cs: nc.vector.tensor_tensor(out=self.g_tmpD[:, cs], in0=pD[:, cs], in1=NEGM, op=ALU.add), reads=[pDk, "cmask"], writes=[("g_tmpD", t)])
            fw.op("act", lambda cs=cs, t=t: nc.scalar.activation(out=self.g_expo[:, cs], in_=self.g_tmpD[:, cs], func=AF.Exp, bias=self.g_ndc[:, t:t + 1]), reads=[("g_tmpD", t), "g_ndc"], writes=[("g_expo", t)])
            if cm <= 2.6:
                continue
            fw.op("pool", lambda cs=cs: nc.gpsimd.tensor_tensor(out=self.g_X[:, cs], in0=self.g_expo[:, cs], in1=STR, op=ALU.mult), reads=[("g_expo", t), "cmask"], writes=[("g_X", t)])
            fw.op("pool", lambda cs=cs: nc.gpsimd.tensor_tensor(out=self.g_X[:, cs], in0=self.g_X[:, cs], in1=self.g_Bbc[:, cs], op=ALU.mult), reads=[("g_X", t), "g_Bbc"], writes=[("g_X", t)])
        if cm <= 2.8:
            return None, None, None, None
        fw.op("pool", lambda: nc.gpsimd.tensor_tensor(out=self.g_qdT[:, 0:N], in0=self.g_qn[:, 0:N], in1=self.g_Ebc[:, 0:N], op=ALU.mult), reads=[("g_qk", 0), "g_Ebc"], writes=["g_qdT"])
        if cm <= 3:
            return None, None, None, None
        pK, pKk = self.psb()
        pQ, pQk = self.psb()
        for t in range(NB):
            cs = slice(t * 128, (t + 1) * 128)
            fw.mm([lambda cs=cs: nc.tensor.matmul(pK[:, cs], lhsT=self.g_kn[:, cs], rhs=self.g_kn[:, cs], start=True, stop=True)], reads=[("g_qk", 1)], writes=[pKk])
            fw.mm([lambda cs=cs: nc.tensor.matmul(pQ[:, cs], lhsT=self.g_kn[:, cs], rhs=self.g_qn[:, cs], start=True, stop=True)], reads=[("g_qk", 1), ("g_qk", 0)], writes=[pQk])
        P0, R0 = self.g_P[0], self.g_R[0]
        for t in range(NB):
            cs = slice(t * 128, (t + 1) * 128)
            fw.op("dve", lambda cs=cs: nc.vector.tensor_tensor(out=P0[:, cs], in0=pK[:, cs], in1=self.g_X[:, cs], op=ALU.mult), reads=[pKk, ("g_X", t)], writes=[("g_P0", t)])
            fw.op("dve", lambda cs=cs: nc.vector.tensor_tensor(out=self.g_attnT[:, cs], in0=pQ[:, cs], in1=self.g_expo[:, cs], op=ALU.mult), reads=[pQk, ("g_expo", t)], writes=[("g_attnT", t)])
            fw.op("pool", lambda cs=cs: nc.gpsimd.tensor_tensor(out=self.g_M[:, cs], in0=IDENT, in1=P0[:, cs], op=ALU.subtract), reads=[("g_P0", t), "cmask"], writes=[("g_M", t)])
        if cm <= 4:
            return None, None, None, None
        pA, pAk = self.psb()
        for t in range(NB):
            cs = slice(t * 128, (t + 1) * 128)
            fw.mm([lambda cs=cs: nc.tensor.transpose(pA[:, cs], P0[:, cs], IDENT)], reads=[("g_P0", t), "cmask"], writes=[pAk])
        self.copy("act", R0[:, 0:N], pA[:, 0:N], reads=[pAk], writes=[("g_R0", t) for t in range(NB)])
        pT, pTk = self.psb()
        pV, pVk = self.psb()
        for t in range(NB):
            cs = slice(t * 128, (t + 1) * 128)
            fw.mm([lambda cs=cs: nc.tensor.transpose(pT[:, cs], self.g_kn[:, cs], IDENT)], reads=[("g_qk", 1), "cmask"], writes=[pTk])
            fw.mm([lambda cs=cs: nc.tensor.transpose(pV[:, cs], ys[2][:, cs], IDENT)], reads=[("g_y", 2), "cmask"], writes=[pVk])
        return pT, pTk, pV, pVk

    def gdn_inverse(self, NB, nlev):
        nc, fw = self.nc, self.fw
        N = NB * 128
        cur = 0
        for lev in range(nlev):
            P, R = self.g_P[cur], self.g_R[cur]
            Pn, Rn = self.g_P[1 - cur], self.g_R[1 - cur]
            last = lev == nlev - 1
            pR, pRk = self.psb()
            for t in range(NB):
                cs = slice(t * 128, (t + 1) * 128)
                fw.mm([lambda cs=cs, P=P, R=R, pR=pR: nc.tensor.matmul(pR[:, cs], lhsT=P[:, cs], rhs=R[:, cs], start=True, stop=True)],
                      reads=[(f"g_P{cur}", t), (f"g_R{cur}", t)], writes=[pRk])
            self.copy("act", Rn[:, 0:N], pR[:, 0:N], reads=[pRk], writes=[(f"g_R{1 - cur}", t) for t in range(NB)])
            if not last:
                pP, pPk = self.psb()
                for t in range(NB):
                    cs = slice(t * 128, (t + 1) * 128)
                    fw.mm([lambda cs=cs, P=P, R=R, pP=pP: nc.tensor.matmul(pP[:, cs], lhsT=R[:, cs], rhs=P[:, cs], start=True, stop=True)],
                          reads=[(f"g_P{cur}", t), (f"g_R{cur}", t)], writes=[pPk])
                self.copy("dve", Pn[:, 0:N], pP[:, 0:N], reads=[pPk], writes=[(f"g_P{1 - cur}", t) for t in range(NB)])
            pM, pMk = self.psb()
            for t in range(NB):
                cs = slice(t * 128, (t + 1) * 128)
                fw.mm([lambda cs=cs, Rn=Rn, pM=pM: nc.tensor.matmul(pM[:, cs], lhsT=Rn[:, cs], rhs=self.g_M[:, cs], start=True, stop=True)],
                      reads=[(f"g_R{1 - cur}", t), ("g_M", t)], writes=[pMk])
            fw.op("dve", lambda pM=pM: nc.vector.tensor_tensor(out=self.g_M[:, 0:N], in0=self.g_M[:, 0:N], in1=pM[:, 0:N], op=ALU.add),
                  reads=[pMk] + [("g_M", t) for t in range(NB)], writes=[("g_M", t) for t in range(NB)])
            cur = 1 - cur

    def gdn_inverse_blocked(self, NB):
        nc, fw = self.nc, self.fw
        N = NB * 128
        IDENT = self.cmask[:, 3, :]
        Af, Tt, Xm, Ao = self.g_tmpD, self.g_X, self.g_Gb, self.g_sq
        P0, R0, M = self.g_P[0], self.g_R[0], self.g_M
        ts = range(NB)
        fw.op("pool", lambda: nc.gpsimd.tensor_copy(out=Af[:, 0:N], in_=R0[:, 0:N]), reads=[("g_R0", t) for t in ts], writes=[("g_tmpD", t) for t in ts])
        for t in ts:
            cs = slice(t * 128, (t + 1) * 128)
            fw.op("dve", lambda cs=cs: nc.vector.tensor_tensor(out=P0[:, cs], in0=P0[:, cs], in1=self.cmask[:, 6, :], op=ALU.mult), reads=[("g_P0", t), "cmask"], writes=[("g_P0", t)])
            fw.op("pool", lambda cs=cs: nc.gpsimd.tensor_tensor(out=M[:, cs], in0=IDENT, in1=P0[:, cs], op=ALU.subtract), reads=[("g_P0", t), "cmask"], writes=[("g_M", t)])
        pA, pAk = self.psb()
        for t in ts:
            cs = slice(t * 128, (t + 1) * 128)
            fw.mm([lambda cs=cs, pA=pA: nc.tensor.transpose(pA[:, cs], P0[:, cs], IDENT)], reads=[("g_P0", t), "cmask"], writes=[pAk])
        self.copy("act", R0[:, 0:N], pA[:, 0:N], reads=[pAk], writes=[("g_R0", t) for t in ts])
        self.gdn_inverse(NB, 2)
        for lv in range(4):
            for t in ts:
                cs = slice(t * 128, (t + 1) * 128)
                fw.op("pool", lambda cs=cs, lv=lv: nc.gpsimd.tensor_tensor(out=Ao[:, cs], in0=Af[:, cs], in1=self.g_mm[:, lv, :], op=ALU.mult), reads=[("g_tmpD", t), "g_mm"], writes=["g_sq"])
            pT, pTk = self.psb()
            for t in ts:
                cs = slice(t * 128, (t + 1) * 128)
                fw.mm([lambda cs=cs, pT=pT: nc.tensor.transpose(pT[:, cs], M[:, cs], IDENT)], reads=[("g_M", t), "cmask"], writes=[pTk])
            self.copy("act", Tt[:, 0:N], pT[:, 0:N], reads=[pTk], writes=[("g_X", t) for t in ts])
            pX, pXk = self.psb()
            for t in ts:
                cs = slice(t * 128, (t + 1) * 128)
                fw.mm([lambda cs=cs, pX=pX: nc.tensor.matmul(pX[:, cs], lhsT=Ao[:, cs], rhs=M[:, cs], start=True, stop=True)], reads=["g_sq", ("g_M", t)], writes=[pXk])
            self.copy("dve", Xm[:, 0:N], pX[:, 0:N], reads=[pXk], writes=[("g_Gb", t) for t in ts])
            pY, pYk = self.psb()
            for t in ts:
                cs = slice(t * 128, (t + 1) * 128)
                fw.mm([lambda cs=cs, pY=pY: nc.tensor.matmul(pY[:, cs], lhsT=Tt[:, cs], rhs=Xm[:, cs], start=True, stop=True)], reads=[("g_X", t), ("g_Gb", t)], writes=[pYk])
            fw.op("dve", lambda pY=pY: nc.vector.tensor_tensor(out=M[:, 0:N], in0=M[:, 0:N], in1=pY[:, 0:N], op=ALU.subtract),
                  reads=[pYk] + [("g_M", t) for t in ts], writes=[("g_M", t) for t in ts])

    def gdn_prompt(self, l):
        cfg, nc, fw = self.cfg, self.nc, self.fw
        NJ, TP = cfg.NJ, cfg.TP
        self.gdn_alloc()
        GXR = self.GXR
        for j in range(NJ):
            fw.cc(lambda j=j: nc.gpsimd.collective_compute("AllGather", ALU.bypass, replica_groups=[[0, 1, 2, 3], [4, 5, 6, 7]],
                                                           ins=[self.gx.ap()[j]], outs=[self.gxg.ap()[j]]), reads=["gx"], writes=["gxg"])
        fw.op("pool", lambda: nc.gpsimd.memset(self.g_S[:], 0.0), writes=["g_S"])
        for c in range(3):
            fw.op("pool", lambda c=c: nc.gpsimd.memset(self.g_xe[1][c][:, 512:515], 0.0), writes=[("g_xe", 1, c)])
        NB = 4
        N = 512
        import os as _os
        stage = int(_os.environ.get("K_GDN_STAGE", "9"))
        for _ in range(int(_os.environ.get("K_DUMMY", "0"))):
            pz, pzk = self.psb()
            fw.mm([lambda pz=pz: nc.tensor.matmul(pz[:, 0:128], lhsT=self.ones_f[:], rhs=self.ones_f[:], start=True, stop=True)], reads=["ones_f"], writes=[pzk])
        for j in range(NJ):
            pi = j % 2
            xe = self.g_xe[pi]
            xep = self.g_xe[1 - pi]
            bg2 = self.g_bg2[pi]
            for c in range(3):
                fw.op("pool", lambda c=c, xe=xe, xep=xep: nc.gpsimd.tensor_copy(out=xe[c][:, 0:3], in_=xep[c][:, 512:515]),
                      reads=[("g_xe", 1 - pi, c)], writes=[("g_xe", pi, c)])
                for rr in range(4):
                    eo = ((j * 4 + rr) * GXR) * 128
                    fw.dma("pool", lambda c=c, rr=rr, eo=eo, xe=xe: nc.gpsimd.indirect_dma_start(
                        out=xe[c][:, 3 + rr * 128:3 + (rr + 1) * 128], out_offset=None, in_=self.gxg.ap().rearrange("j r c -> (j r) c"),
                        in_offset=bass.IndirectOffsetOnAxis(ap=self.idxt[:, c:c + 1], axis=0), element_offset=eo),
                        reads=["gxg", "idxt"], writes=[("g_xe", pi, c)])
            for rr in range(4):
                eo = ((j * 4 + rr) * GXR) * 128
                fw.dma("pool", lambda rr=rr, eo=eo, bg2=bg2: nc.gpsimd.indirect_dma_start(
                    out=bg2[0:2, rr * 128:(rr + 1) * 128], out_offset=None, in_=self.gxg.ap().rearrange("j r c -> (j r) c"),
                    in_offset=bass.IndirectOffsetOnAxis(ap=self.idxt[0:2, 3:4], axis=0), element_offset=eo),
                    reads=["gxg", "idxt"], writes=[("g_bg2", pi)])
            if stage <= 1:
                continue
            for c in range(3):
                y = self.g_y[c]
                fw.op("dve", lambda c=c, y=y, xe=xe: nc.vector.tensor_scalar(out=y[:, 0:N], in0=xe[c][:, 0:N], scalar1=self.cwH[:, l, c, 0:1], scalar2=None, op0=ALU.mult),
                      reads=[("g_xe", pi, c), "cwH"], writes=[("g_y", c)])
                for jj in range(1, 4):
                    fw.op("dve", lambda c=c, y=y, xe=xe, jj=jj: nc.vector.scalar_tensor_tensor(out=y[:, 0:N], in0=xe[c][:, jj:jj + N], scalar=self.cwH[:, l, c, jj:jj + 1], in1=y[:, 0:N], op0=ALU.mult, op1=ALU.add),
                          reads=[("g_xe", pi, c), "cwH", ("g_y", c)], writes=[("g_y", c)])
            fw.lastw[("bsrc", "p")] = fw.lastw.get(("g_bg2", pi))
            fw.lastw[("gsrc", "p")] = fw.lastw.get(("g_bg2", pi))
            fw.readers[("bsrc", "p")] = []
            fw.readers[("gsrc", "p")] = []
            if stage <= 2:
                continue
            pT, pTk, pV, pVk = self.gdn_chunk_math(NB, 6, 0, 1, 2, bg2, 0, bg2, 1, 2, self.g_y, "p")
            if pT is None:
                continue
            fw.readers.setdefault(("g_bg2", pi), []).extend(fw.readers.get(("bsrc", "p"), []) + fw.readers.get(("gsrc", "p"), []))
            pl, plk = self.psb()
            for t in range(NB):
                fw.mm([lambda t=t, pl=pl: nc.tensor.matmul(pl[:, t:t + 1], lhsT=self.ones_f[:], rhs=self.g_bgc[:, t, 1:2], start=True, stop=True)], reads=[("g_bgc", t), "ones_f"], writes=[plk])
            self.copy("dve", self.g_dl[:, 0:NB], pl[:, 0:NB], reads=[plk], writes=["g_dl"])
            fw.op("act", lambda: nc.scalar.activation(out=self.g_gl[:, 0:NB], in_=self.g_dl[:, 0:NB], func=AF.Exp), reads=["g_dl"], writes=["g_gl"])
            fw.op("dve", lambda: nc.vector.tensor_tensor(out=self.g_ekd[:, 0:NB], in0=self.g_dl[:, 0:NB], in1=self.g_dc[:, 0:NB], op=ALU.subtract), reads=["g_dl", "g_dc"], writes=["g_ekd"])
            fw.op("act", lambda: nc.scalar.activation(out=self.g_ekd[:, 0:NB], in_=self.g_ekd[:, 0:NB], func=AF.Exp), reads=["g_ekd"], writes=["g_ekd"])
            for t in range(NB):
                cs = slice(t * 128, (t + 1) * 128)
                fw.op("dve", lambda cs=cs, t=t, pT=pT: nc.vector.tensor_scalar(out=self.g_kd[:, cs], in0=pT[:, cs], scalar1=self.g_ekd[:, t:t + 1], scalar2=None, op0=ALU.mult), reads=[pTk, "g_ekd"], writes=[("g_kd", t)])
                fw.op("dve", lambda cs=cs, t=t, pT=pT: nc.vector.tensor_scalar(out=self.g_Kbd[:, cs], in0=pT[:, cs], scalar1=self.g_bgc[:, t, 0:1], scalar2=self.g_edc[:, t:t + 1], op0=ALU.mult, op1=ALU.mult), reads=[pTk, ("g_bgc", t), "g_edc"], writes=[("g_Kbd", t)])
                fw.op("act", lambda cs=cs, t=t, pV=pV: nc.scalar.activation(out=self.g_Vb[:, cs], in_=pV[:, cs], func=AF.Copy, scale=self.g_bgc[:, t, 0:1]), reads=[pVk, ("g_bgc", t)], writes=[("g_Vb", t)])
            if stage <= 3:
                continue
            self.gdn_inverse_blocked(NB)
            if stage <= 4:
                continue
            for _ in range(int(_os.environ.get("K_SHIFT", "0"))):
                self.psb()
            pu, puk = self.psb()
            pw, pwk = self.psb()
            for t in range(NB):
                cs = slice(t * 128, (t + 1) * 128)
                if _os.environ.get("K_ALT") == "1":
                    fw.mm([lambda cs=cs: nc.tensor.matmul(pu[:, cs], lhsT=self.g_M[:, cs], rhs=self.g_M[:, cs], start=True, stop=True)], reads=[("g_M", t), ("g_Vb", t)], writes=[puk])
                    continue
                if _os.environ.get("K_ALT") == "3":
                    if t == 0 and j == 0:
                        fw.mm([lambda cs=cs: nc.tensor.matmul(pu[:, cs], lhsT=self.ones_f[:], rhs=self.ones_f[:], start=True, stop=True)], reads=["ones_f"], writes=[puk])
                    continue
                if _os.environ.get("K_ALT") == "4":
                    if t == 0 and j == 0:
                        fw.mm([lambda cs=cs: nc.tensor.matmul(pu[:, cs], lhsT=self.ones_bf[:], rhs=self.ones_bf[:], start=True, stop=True)], reads=["ones_bf"], writes=[puk])
                    continue
                if _os.environ.get("K_ALT") == "5":
                    if t == 0 and j == 1:
                        fw.mm([lambda cs=cs: nc.tensor.matmul(pu[:, cs], lhsT=self.ones_f[:], rhs=self.ones_f[:], start=True, stop=True)], reads=["ones_f"], writes=[puk])
                    continue
                if _os.environ.get("K_ALT") == "2":
                    fw.mm([lambda cs=cs: nc.tensor.matmul(pu[:, cs], lhsT=self.ones_f[:], rhs=self.ones_f[:], start=True, stop=True)], reads=["ones_f"], writes=[puk])
                    continue
                fw.mm([lambda cs=cs, pu=pu: nc.tensor.matmul(pu[:, cs], lhsT=self.g_M[:, cs], rhs=self.g_Vb[:, cs], start=True, stop=True)], reads=[("g_M", t), ("g_Vb", t)], writes=[puk])
                if _os.environ.get("K_NOPW"):
                    continue
                fw.mm([lambda cs=cs, pw=pw: nc.tensor.matmul(pw[:, cs], lhsT=self.g_Kbd[:, cs], rhs=self.g_M[:, cs], start=True, stop=True)], reads=[("g_M", t), ("g_Kbd", t)], writes=[pwk])
            if not _os.environ.get("K_NOCP"):
                self.copy("act", self.g_u[:, 0:N], pu[:, 0:N], reads=[puk], writes=["g_u"])
                self.copy("dve", self.g_wT[:, 0:N], pw[:, 0:N], reads=[pwk], writes=["g_wT"])
            sq_ = int(_os.environ.get("K_SEQ", "9"))
            for t in range(NB):
                if sq_ <= 1:
                    continue
                cs = slice(t * 128, (t + 1) * 128)
                pws, pwsk = self.psb()
                fw.mm([lambda cs=cs, pws=pws: nc.tensor.matmul(pws[:, 0:128], lhsT=self.g_wT[:, cs], rhs=self.g_S[:], start=True, stop=True)], reads=["g_wT", "g_S"], writes=[pwsk])
                fw.op("dve", lambda cs=cs, pws=pws: nc.vector.tensor_tensor(out=self.g_vnew[:], in0=self.g_u[:, cs], in1=pws[:, 0:128], op=ALU.subtract), reads=["g_u", pwsk], writes=["g_vnew"])
                if sq_ <= 2:
                    continue
                po, pok = self.psb()
                fw.mm([lambda cs=cs, po=po: nc.tensor.matmul(po[:, 0:128], lhsT=self.g_qdT[:, cs], rhs=self.g_S[:], start=True, stop=False),
                       lambda cs=cs, po=po: nc.tensor.matmul(po[:, 0:128], lhsT=self.g_attnT[:, cs], rhs=self.g_vnew[:], start=False, stop=True)],
                      reads=["g_qdT", "g_S", ("g_attnT", t), "g_vnew"], writes=[pok])
                if sq_ <= 3:
                    continue
                psd, psdk = self.psb()
                fw.mm([lambda cs=cs, psd=psd: nc.tensor.matmul(psd[:, 0:128], lhsT=self.g_kd[:, cs], rhs=self.g_vnew[:], start=True, stop=True)], reads=[("g_kd", t), "g_vnew"], writes=[psdk])
                fw.op("dve", lambda t=t, psd=psd: nc.vector.scalar_tensor_tensor(out=self.g_S[:], in0=self.g_S[:], scalar=self.g_gl[:, t:t + 1], in1=psd[:, 0:128], op0=ALU.mult, op1=ALU.add),
                      reads=["g_S", "g_gl", psdk], writes=["g_S"])
                if sq_ <= 4:
                    continue
                fw.op("act", lambda t=t, po=po: nc.scalar.activation(out=self.g_junk[:], in_=po[:, 0:128], func=AF.Square, accum_out=self.g_ss[:, t:t + 1]), reads=[pok], writes=["g_junk", ("g_ss", t)])
                fw.op("act", lambda t=t: nc.scalar.activation(out=self.g_ss[:, t:t + 1], in_=self.g_ss[:, t:t + 1], func=AF.Ln, bias=self.epsc[:, 0:1], scale=1.0 / 128), reads=[("g_ss", t), "epsc"], writes=[("g_ss", t)])
                fw.op("act", lambda t=t: nc.scalar.activation(out=self.g_ss[:, t:t + 1], in_=self.g_ss[:, t:t + 1], func=AF.Exp, scale=-0.5), reads=[("g_ss", t)], writes=[("g_ss", t)])
                fw.op("act", lambda t=t, po=po: nc.scalar.activation(out=self.g_on[:], in_=po[:, 0:128], func=AF.Copy, scale=self.g_ss[:, t:t + 1]), reads=[pok, ("g_ss", t)], writes=["g_on"])
                if sq_ <= 5:
                    continue
                pot, potk = self.psb()
                fw.mm([lambda pot=pot: nc.tensor.transpose(pot[:, 0:128], self.g_on[:], self.cmask[:, 3, :])], reads=["g_on", "cmask"], writes=[potk])
                self.copy("dve", self.g_oT[:, cs], pot[:, 0:128], reads=[potk], writes=[("g_oT", t)])
                og, oj = j // self.OG, j % self.OG
                fw.dma("sp", lambda t=t, cs=cs, og=og, oj=oj: nc.sync.dma_start(out=self.ox.ap()[og, t * 128:(t + 1) * 128, oj * 128:(oj + 1) * 128], in_=self.g_oT[:, cs]),
                       reads=[("g_oT", t)], writes=["ox"])
        fw.dma("sp", lambda: nc.sync.dma_start(out=self.gdnP.ap()[l], in_=self.g_S[:]), reads=["g_S"], writes=["gdnP_out"])
        if stage <= 5:
            return
        for og in range(self.NOG):
            fw.cc(lambda og=og: nc.gpsimd.collective_compute("AllGather", ALU.bypass, replica_groups=[[0, 1, 2, 3], [4, 5, 6, 7]],
                                                             ins=[self.ox.ap()[og]], outs=[self.oxg.ap()[og]]), reads=["ox"], writes=["oxg"])
        OW = self.OG * 128
        for h in range(4):
            for og in range(self.NOG):
                fw.dma("pool", lambda h=h, og=og: nc.gpsimd.indirect_dma_start(
                    out=self.g_ostg[:, 0:OW], out_offset=None, in_=self.oxg.ap().rearrange("g r c -> (g r) c"),
                    in_offset=bass.IndirectOffsetOnAxis(ap=self.idxt[:, 4 + h:5 + h], axis=0), element_offset=og * 2048 * OW),
                    reads=["oxg", "idxt"], writes=["g_ostg"])
                fw.dma("sp", lambda h=h, og=og: nc.sync.dma_start(out=self.g_gate[:, 0:OW], in_=self.gate_d.ap()[h, :, og * OW:(og + 1) * OW]), reads=["gate_d"], writes=["g_gate"])
                fw.op("dve", lambda h=h: nc.vector.scalar_tensor_tensor(out=self.g_gate[:, 0:OW], in0=self.g_ostg[:, 0:OW], scalar=self.gnv[:, l:l + 1], in1=self.g_gate[:, 0:OW], op0=ALU.mult, op1=ALU.mult),
                      reads=["g_ostg", "gnv", "g_gate"], writes=["g_gate"])
                fw.dma("sp", lambda h=h, og=og: nc.sync.dma_start(out=self.o_d.ap()[h, :, og * OW:(og + 1) * OW], in_=self.g_gate[:, 0:OW]), reads=["g_gate"], writes=["o_d"])


    def nsa_common_alloc(self, l, NCp):
        cfg, nc, fw = self.cfg, self.nc, self.fw
        a = self.aalloc
        gw = 128 * (cfg.NT if NCp == 512 else cfg.NPG + 1)
        n_G = self.n_G = a("n_G", [128, gw], BF16)
        fw.dma("sp", lambda: nc.sync.dma_start(out=n_G, in_=self.G_in.ap()[:, 0:gw]), writes=["n_G"])
        n_idb = self.n_idb = a("n_idb", [128, 128], BF16)
        self.copy("dve", n_idb, self.cmask[:, 3, :], reads=["cmask"], writes=["n_idb"])
        n_onesb = self.n_onesb = a("n_onesb", [1, 512], BF16)
        fw.op("pool", lambda: nc.gpsimd.memset(n_onesb, 1.0), writes=["n_onesb"])
        n_ngr = self.n_ngr = a("n_ngr", [128, 64], F32)
        fw.dma("sp", lambda: nc.sync.dma_start(out=n_ngr, in_=self.ng_in.ap()[:, l, :]), writes=["n_ngr"])
        n_wb = self.n_wb = a("n_wb", [64, 64, 64], BF16)
        n_pb = self.n_pb = a("n_pb", [64, 64], BF16)
        n_brow = self.n_brow = a("n_brow", [1, 2, 64], BF16)
        mark0 = self.abump
        wf = a("n_wf", [64, 64, 64], F32)
        pf = a("n_pf", [64, 64], F32)
        fw.dma("sp", lambda: nc.sync.dma_start(out=wf, in_=self.cw_in.ap()[:, l, :, :]), writes=["n_wf"])
        fw.dma("sp", lambda: nc.sync.dma_start(out=pf, in_=self.pe_in.ap()[:, l, :]), writes=["n_pf"])
        self.copy("dve", n_wb, wf, reads=["n_wf"], writes=["n_wb"])
        self.copy("dve", n_pb, pf, reads=["n_pf"], writes=["n_pb"])
        for c in range(2):
            pt, pk = self.psb()
            fw.mm([lambda lc=lc, c=c, pt=pt: nc.tensor.matmul(pt[0:1, 0:64], lhsT=n_pb[:, 2 * lc + c:2 * lc + c + 1], rhs=n_wb[:, 2 * lc + c, :], start=(lc == 0), stop=(lc == 31))
                   for lc in range(32)], reads=["n_pb", "n_wb"], writes=[pk])
            self.copy("act", n_brow[0:1, c, :], pt[0:1, 0:64], reads=[pk], writes=["n_brow"])
        fw.barrier()
        self.abump = mark0

    def nsa_compress(self, g, ktile, vtile, nblk, KcT, Vc1, tag):
        n_brow = self.n_brow
        n_onesb = self.n_onesb
        n_wb = self.n_wb
        nc, fw = self.nc, self.fw
        pt, pk = self.psb()
        fns = [lambda lc=lc, pt=pt: nc.tensor.matmul(pt[0:64, 0:nblk], lhsT=n_wb[:, 2 * lc, :], rhs=ktile[:, lc:lc + 16 * (nblk - 1) + 1:16], start=(lc == 0), stop=False)
               for lc in range(32)]
        fns.append(lambda pt=pt: nc.tensor.matmul(pt[0:64, 0:nblk], lhsT=n_brow[0:1, 0, :], rhs=n_onesb[0:1, 0:nblk], start=False, stop=True))
        fw.mm(fns, reads=[("n_raw", tag, 0), "n_wb", "n_brow", "n_onesb"], writes=[pk])
        self.copy("act", KcT[0:64, 0:nblk], pt[0:64, 0:nblk], reads=[pk], writes=[("n_KcT", g)])
        ncc = (nblk + 127) // 128
        for cc in range(ncc):
            nb = min(128, nblk - cc * 128)
            pt, pk = self.psb()
            fns = [lambda lc=lc, pt=pt, cc=cc, nb=nb: nc.tensor.matmul(pt[0:nb, 0:64], lhsT=vtile[:, 2048 * cc + lc:2048 * cc + lc + 16 * (nb - 1) + 1:16], rhs=n_wb[:, 2 * lc + 1, :], start=(lc == 0), stop=False)
                   for lc in range(32)]
            fns.append(lambda pt=pt, nb=nb: nc.tensor.matmul(pt[0:nb, 0:64], lhsT=n_onesb[0:1, 0:nb], rhs=n_brow[0:1, 1, :], start=False, stop=True))
            fw.mm(fns, reads=[("n_raw", tag, 1), "n_wb", "n_brow", "n_onesb"], writes=[pk])
            self.copy("dve", Vc1[0:nb, cc, 0:64], pt[0:nb, 0:64], reads=[pk], writes=[("n_Vc1", g)])

    def nsa_block(self, NQ, NCp, NBLK, qaug, KcT, Vc1, cmk, fbt, gates, slc_chunks, win_chunks, out_cols, tag):
        n_G = self.n_G
        n_e = self.n_e
        n_eT = self.n_eT
        n_idb = self.n_idb
        n_imp = self.n_imp
        n_junk = self.n_junk
        n_m8 = self.n_m8
        n_ngr = self.n_ngr
        n_nsT = self.n_nsT
        n_nsb = self.n_nsb
        n_obr = self.n_obr
        n_on = self.n_on
        n_pT = self.n_pT
        n_pool = self.n_pool
        n_rinv = self.n_rinv
        n_rs = self.n_rs
        n_sc2 = self.n_sc2
        n_score = self.n_score
        n_ss = self.n_ss
        nc, fw = self.nc, self.fw
        NBLKi = NCp // 4
        NCc = NCp // 128
        IDb = n_idb
        obr = n_obr
        n_ost = self.n_ost
        for g in range(2):
            for r in range(4):
                h = 4 * g + r
                ps, pk = self.psb()
                fw.mm([lambda ps=ps, h=h, g=g: nc.tensor.matmul(ps[0:NQ, 0:NCp], lhsT=qaug[:, h, :], rhs=KcT[g][:, 0:NCp], start=True, stop=False),
                       lambda ps=ps: nc.tensor.matmul(ps[0:NQ, 0:NCp], lhsT=IDb[0:NQ, 0:NQ], rhs=cmk[0:NQ, 0:NCp], start=False, stop=True)],
                      reads=[("n_qaug", tag), ("n_KcT", g), "n_idb", ("n_cmk", tag)], writes=[pk])
                fw.op("act", lambda ps=ps: nc.scalar.activation(out=n_e[0:NQ, 0:NCp], in_=ps[0:NQ, 0:NCp], func=AF.Exp, accum_out=n_rs[0:NQ, 0:1]),
                      reads=[pk], writes=["n_e", "n_rs"])
                fw.op("dve", lambda: nc.vector.tensor_scalar(out=n_rs[0:NQ, 0:1], in0=n_rs[0:NQ, 0:1], scalar1=1e-30, scalar2=None, op0=ALU.max), reads=["n_rs"], writes=["n_rs"])
                fw.op("dve", lambda: nc.vector.reciprocal(out=n_rinv[0:NQ, 0:1], in_=n_rs[0:NQ, 0:1]), reads=["n_rs"], writes=["n_rinv"])
                fw.op("dve", lambda: nc.vector.tensor_reduce(out=n_pool[0:NQ, 0:NBLKi], in_=n_e[0:NQ, 0:NCp].rearrange("p (b f) -> p b f", f=4), axis=AX.X, op=ALU.add),
                      reads=["n_e"], writes=["n_pool"])
                if r == 0:
                    fw.op("dve", lambda: nc.vector.tensor_scalar(out=n_imp[0:NQ, 0:NBLKi], in0=n_pool[0:NQ, 0:NBLKi], scalar1=n_rinv[0:NQ, 0:1], scalar2=None, op0=ALU.mult),
                          reads=["n_pool", "n_rinv"], writes=["n_imp"])
                else:
                    fw.op("dve", lambda: nc.vector.scalar_tensor_tensor(out=n_imp[0:NQ, 0:NBLKi], in0=n_pool[0:NQ, 0:NBLKi], scalar=n_rinv[0:NQ, 0:1], in1=n_imp[0:NQ, 0:NBLKi], op0=ALU.mult, op1=ALU.add),
                          reads=["n_pool", "n_rinv", "n_imp"], writes=["n_imp"])
                pT, pTk = self.psb()
                pTb = pT.bitcast(BF16)
                for cc in range(NCc):
                    fw.mm([lambda cc=cc, pTb=pTb: nc.tensor.transpose(pTb[:, cc * NQ:(cc + 1) * NQ], n_e[0:NQ, cc * 128:(cc + 1) * 128], IDb[0:NQ, 0:NQ])], reads=["n_e", "n_idb"], writes=[pTk])
                self.copy("act", n_eT[:, 0:NCc * NQ], pTb[:, 0:NCc * NQ], reads=[pTk], writes=["n_eT"])
                po, pok = self.psb()
                fw.mm([lambda cc=cc, po=po, g=g: nc.tensor.matmul(po[0:NQ, 0:64], lhsT=n_eT[:, cc * NQ:(cc + 1) * NQ], rhs=Vc1[g][:, cc, 0:64], start=(cc == 0), stop=(cc == NCc - 1))
                       for cc in range(NCc)], reads=["n_eT", ("n_Vc1", g)], writes=[pok])
                fw.op("act", lambda po=po, h=h: nc.scalar.activation(out=obr[0:NQ, h, 0, :], in_=po[0:NQ, 0:64], func=AF.Copy, scale=n_rinv[0:NQ, 0:1]), reads=[pok, "n_rinv"], writes=[("n_obr", h, 0)])
            sc = n_score
            self.copy("pool", sc[0:NQ, 0:NBLK], fbt[0:NQ, 0:NBLK], reads=[("n_fbt", tag)], writes=["n_score"])
            fw.op("dve", lambda: nc.vector.tensor_tensor(out=sc[0:NQ, 0:NBLKi], in0=sc[0:NQ, 0:NBLKi], in1=n_imp[0:NQ, 0:NBLKi], op=ALU.add), reads=["n_score", "n_imp"], writes=["n_score"])
            fw.op("dve", lambda: nc.vector.max(out=n_m8[0:NQ, 0:8], in_=sc[0:NQ, 0:NBLK]), reads=["n_score"], writes=["n_m8"])
            fw.op("dve", lambda: nc.vector.match_replace(out=n_sc2[0:NQ, 0:NBLK], in_to_replace=n_m8[0:NQ, 0:8], in_values=sc[0:NQ, 0:NBLK], imm_value=-1e30), reads=["n_score", "n_m8"], writes=["n_sc2"])
            fw.op("dve", lambda: nc.vector.max(out=n_m8[0:NQ, 8:16], in_=n_sc2[0:NQ, 0:NBLK]), reads=["n_sc2"], writes=["n_m8"])
            fw.op("dve", lambda: nc.vector.tensor_scalar(out=n_sc2[0:NQ, 0:NBLK], in0=sc[0:NQ, 0:NBLK], scalar1=n_m8[0:NQ, 15:16], scalar2=None, op0=ALU.is_ge), reads=["n_score", "n_m8"], writes=["n_sc2"])
            fw.op("dve", lambda: nc.vector.tensor_scalar(out=n_nsb[0:NQ, 0:NBLK], in0=n_sc2[0:NQ, 0:NBLK], scalar1=-1.0, scalar2=-NEG, op0=ALU.add, op1=ALU.mult), reads=["n_sc2"], writes=["n_nsb"])
            pT, pTk = self.psb()
            pTb = pT.bitcast(BF16)
            fw.mm([lambda pTb=pTb: nc.tensor.transpose(pTb[0:NBLK, 0:NQ], n_nsb[0:NQ, 0:NBLK], IDb[0:NQ, 0:NQ])], reads=["n_nsb", "n_idb"], writes=[pTk])
            self.copy("act", n_nsT[0:NBLK, 0:NQ], pTb[0:NBLK, 0:NQ], reads=[pTk], writes=["n_nsT"])
            for bi, chunks in ((1, slc_chunks), (2, win_chunks)):
                acc, acck = self.ps[6 + (bi - 1)], ("ps", 6 + (bi - 1))
                fw.op("dve", lambda acc=acc: nc.vector.memset(acc[0:NQ, 0:260], 0.0), writes=[acck])
                for (KT, V, gcol, mk, mkey) in chunks:
                    ps, pk = self.psb()
                    pairs = [(KT[g], qaug[:, 4 * g:4 * g + 4, :])]
                    rd = [("n_qaug", tag), ("n_kt", tag, bi)]
                    if bi == 1:
                        pairs.append((n_G[0:NBLK, gcol:gcol + 128], n_nsT[0:NBLK, 0:NQ].unsqueeze(1).to_broadcast([NBLK, 4, NQ])))
                        rd += ["n_G", "n_nsT"]
                    if mk is not None:
                        pairs.append((IDb[:, :], mk))
                        rd += ["n_idb", mkey]
                    fns = [lambda ps=ps, a_=a_, b_=b_, i=i, n_=len(pairs): nc.tensor.matmul(ps[:, 0:4 * NQ], lhsT=a_, rhs=b_, start=(i == 0), stop=(i == n_ - 1))
                           for i, (a_, b_) in enumerate(pairs)]
                    fw.mm(fns, reads=rd, writes=[pk])
                    fw.op("act", lambda ps=ps: nc.scalar.activation(out=n_pT[:, 0:4 * NQ], in_=ps[:, 0:4 * NQ], func=AF.Exp), reads=[pk], writes=["n_pT"])
                    fw.mm([lambda r=r, V=V, acc=acc, g=g: nc.tensor.matmul(acc[0:NQ, r * 65:(r + 1) * 65], lhsT=n_pT[:, r * NQ:(r + 1) * NQ], rhs=V[g], start=False, stop=False, skip_group_check=True)
                           for r in range(4)], reads=["n_pT", ("n_vt", tag, bi)], writes=[acck])
                for r in range(4):
                    h = 4 * g + r
                    fw.op("dve", lambda acc=acc, r=r: nc.vector.reciprocal(out=n_rinv[0:NQ, 0:1], in_=acc[0:NQ, r * 65 + 64:r * 65 + 65]), reads=[acck], writes=["n_rinv"])
                    fw.op("act", lambda acc=acc, r=r, h=h, bi=bi: nc.scalar.activation(out=obr[0:NQ, h, bi, :], in_=acc[0:NQ, r * 65:r * 65 + 64], func=AF.Copy, scale=n_rinv[0:NQ, 0:1]), reads=[acck, "n_rinv"], writes=[("n_obr", h, bi)])
        on = n_on
        for h in range(8):
            fw.op("dve", lambda h=h: nc.vector.tensor_scalar(out=on[0:NQ, h, :], in0=obr[0:NQ, h, 0, :], scalar1=gates[0:NQ, 3 * h:3 * h + 1], scalar2=None, op0=ALU.mult), reads=[("n_obr", h, 0), ("n_gates", tag)], writes=[("n_on", h)])
            for c in (1, 2):
                fw.op("dve", lambda h=h, c=c: nc.vector.scalar_tensor_tensor(out=on[0:NQ, h, :], in0=obr[0:NQ, h, c, :], scalar=gates[0:NQ, 3 * h + c:3 * h + c + 1], in1=on[0:NQ, h, :], op0=ALU.mult, op1=ALU.add),
                      reads=[("n_obr", h, c), ("n_gates", tag), ("n_on", h)], writes=[("n_on", h)])
            fw.op("act", lambda h=h: nc.scalar.activation(out=n_junk[0:NQ, 0:64], in_=on[0:NQ, h, :], func=AF.Square, accum_out=n_ss[0:NQ, h:h + 1]), reads=[("n_on", h)], writes=["n_junk", ("n_ss", h)])
        fw.op("act", lambda: nc.scalar.activation(out=n_ss[0:NQ, 0:8], in_=n_ss[0:NQ, 0:8], func=AF.Ln, bias=self.epsc[0:NQ, 0:1], scale=1.0 / 64), reads=[("n_ss", h) for h in range(8)] + ["epsc"], writes=[("n_ss", h) for h in range(8)])
        fw.op("act", lambda: nc.scalar.activation(out=n_ss[0:NQ, 0:8], in_=n_ss[0:NQ, 0:8], func=AF.Exp, scale=-0.5), reads=[("n_ss", h) for h in range(8)], writes=[("n_ss", h) for h in range(8)])
        for h in range(8):
            fw.op("dve", lambda h=h: nc.vector.scalar_tensor_tensor(out=on[0:NQ, h, :], in0=on[0:NQ, h, :], scalar=n_ss[0:NQ, h:h + 1], in1=n_ngr[0:NQ, :], op0=ALU.mult, op1=ALU.mult),
                  reads=[("n_on", h), ("n_ss", h), "n_ngr"], writes=[("n_on", h)])
        for k in range(4):
            pt, pk = self.psb()
            fw.mm([lambda k=k, pt=pt: nc.tensor.transpose(pt[:, 0:NQ], on[0:NQ, 2 * k:2 * k + 2, :].rearrange("p a b -> p (a b)"), self.cmask[0:NQ, 3, 0:NQ])], reads=[("n_on", 2 * k), ("n_on", 2 * k + 1), "cmask"], writes=[pk])
            self.copy("act" if k % 2 else "dve", n_ost[:, k, 0:NQ], pt[:, 0:NQ], reads=[pk], writes=[("n_ost", k)])
            fw.dma("sp", lambda k=k: nc.sync.dma_start(out=self.o_d.ap()[4 + k, :, out_cols], in_=n_ost[:, k, 0:NQ]), reads=[("n_ost", k)], writes=["o_d"])

    def nsa_work_alloc(self, NQmax=128):
        a = self.aalloc
        self.n_e = a("n_e", [128, 512], BF16)
        self.n_rs = a("n_rs", [128, 1], F32)
        self.n_rinv = a("n_rinv", [128, 1], F32)
        self.n_pool = a("n_pool", [128, 128], F32)
        self.n_imp = a("n_imp", [128, 128], F32)
        self.n_eT = a("n_eT", [128, 512], BF16)
        self.n_score = a("n_score", [128, 128], F32)
        self.n_sc2 = a("n_sc2", [128, 128], F32)
        self.n_m8 = a("n_m8", [128, 16], F32)
        self.n_nsb = a("n_nsb", [128, 128], BF16)
        self.n_nsT = a("n_nsT", [128, 128], BF16)
        self.n_pT = a("n_pT", [128, 512], BF16)
        self.n_obr = a("n_obr", [128, 8, 3, 64], F32)
        self.n_on = a("n_on", [128, 8, 64], F32)
        self.n_junk = a("n_junk", [128, 64], F32)
        self.n_ss = a("n_ss", [128, 8], F32)
        self.n_ost = a("n_ost", [128, 4, 128], BF16)

    def nsa_prompt(self, l):
        cfg, nc, fw = self.cfg, self.nc, self.fw
        NJ, NT, TP = cfg.NJ, cfg.NT, cfg.TP
        a = self.aalloc
        self.areset()
        self.ps_lim = 6
        self.psn = 0
        groups = [[0, 1, 2, 3], [4, 5, 6, 7]]
        for q in range(self.NKG):
            fw.cc(lambda q=q: nc.gpsimd.collective_compute("AllGather", ALU.bypass, replica_groups=groups, ins=[self.kx.ap()[q]], outs=[self.kxg.ap()[q]]), reads=["kx"], writes=["kxg"])
        for q in range(self.NVG):
            fw.cc(lambda q=q: nc.gpsimd.collective_compute("AllGather", ALU.bypass, replica_groups=groups, ins=[self.vx.ap()[q]], outs=[self.vxg.ap()[q]]), reads=["vx"], writes=["vxg"])
        self.nsa_common_alloc(l, 512)
        KG, VGp = self.KG, self.VGp

        def ksrc(row0, t):
            rr, j = t % 4, t // 4
            q, jj = j // KG, j % KG
            return self.kxg.ap()[q].rearrange("(r j f) c -> r j f c", r=4, j=KG)[rr, jj, row0:row0 + 64, :]

        def vsrc(col0, t):
            rr, j = t % 4, t // 4
            q, jj = j // VGp, j % VGp
            return self.vxg.ap()[q].rearrange("(r j t) f -> r j t f", r=4, j=VGp)[rr, jj, :, col0:col0 + 64]

        hflat = self.hT[:, :, :].rearrange("p a b -> p (a b)")
        assert 2 * NT * 128 <= KC * cfg.TL
        sKT = [hflat[0:68, g * NT * 128:(g + 1) * NT * 128].rearrange("p (t c) -> p t c", c=128) for g in range(2)]
        sV = [a(f"n_sV{g}", [128, NT, 65], BF16) for g in range(2)]
        for g in range(2):
            fw.dma("sp", lambda g=g: nc.sync.dma_start(out=sKT[g][64:68, :, :], in_=self.kaug_in.ap().rearrange("r (t c) -> r t c", c=128)), writes=[("n_kt", "p", 1)])
            fw.op("pool", lambda g=g: nc.gpsimd.memset(sV[g][:, :, 64:65], 1.0), writes=[("n_vt", "p", 1)])
            for t in range(NT):
                fw.dma("sp", lambda g=g, t=t: nc.sync.dma_start(out=sKT[g][0:64, t, :], in_=ksrc(256 + 64 * g, t)), reads=["kxg"], writes=[("n_kt", "p", 1)])
                fw.dma("sp", lambda g=g, t=t: nc.sync.dma_start(out=sV[g][:, t, 0:64], in_=vsrc(64 * g, t)), reads=["vxg"], writes=[("n_vt", "p", 1)])
        KcT = [a(f"n_KcT{g}", [68, 512], BF16) for g in range(2)]
        Vc1 = [a(f"n_Vc1{g}", [128, 4, 64], BF16) for g in range(2)]
        cm1 = a("n_cm1", [128, 4, 128], BF16)
        wm1 = a("n_wm1", [128, 2, 8, 128], BF16)
        mark = self.abump
        raw = [a(f"n_raw{c}", [64, NT * 128 + 16], BF16) for c in range(2)]
        nblk = NT * 8 - 1
        for g in range(2):
            fw.dma("sp", lambda g=g: nc.sync.dma_start(out=KcT[g][64:68, :], in_=self.caug_in.ap()), writes=[("n_KcT", g)])
            fw.op("pool", lambda g=g: nc.gpsimd.memset(KcT[g][0:64, :], 0.0), writes=[("n_KcT", g)])
            fw.op("pool", lambda g=g: nc.gpsimd.memset(Vc1[g][:, :, :], 0.0), writes=[("n_Vc1", g)])
            for c in range(2):
                fw.op("pool", lambda c=c: nc.gpsimd.memset(raw[c][:, NT * 128:NT * 128 + 16], 0.0), writes=[("n_raw", "p", c)])
                for t in range(NT):
                    fw.dma("sp", lambda g=g, c=c, t=t: nc.sync.dma_start(out=raw[c][:, t * 128:(t + 1) * 128], in_=ksrc(128 * c + 64 * g, t)), reads=["kxg"], writes=[("n_raw", "p", c)])
            self.nsa_compress(g, raw[0], raw[1], nblk, KcT[g], Vc1[g], "p")
        fw.barrier()
        self.abump = mark
        self.nsa_work_alloc()
        fw.dma("sp", lambda: nc.sync.dma_start(out=cm1, in_=self.cm_in.ap()), writes=["n_cm1"])
        fw.dma("sp", lambda: nc.sync.dma_start(out=wm1, in_=self.wm_in.ap()), writes=["n_wm1"])
        qaug = a("n_qaug", [68, 8, 128], BF16)
        cmk = a("n_cmk", [128, 512], BF16)
        fbt = a("n_fbt", [128, 128], F32)
        gts = a("n_gates", [128, 24], F32)
        wKT = [a(f"n_wKT{g}", [68, 8, 128], BF16) for g in range(2)]
        wV = [a(f"n_wV{g}", [128, 8, 65], BF16) for g in range(2)]
        for j in range(NJ):
            cols = slice(j * 128, (j + 1) * 128)
            fw.dma("sp", lambda j=j: nc.sync.dma_start(out=qaug[0:64, :, :], in_=self.qx.ap()[:, :, j * 128:(j + 1) * 128].rearrange("h d c -> d h c")), reads=["qx"], writes=[("n_qaug", "p")])
            fw.dma("sp", lambda j=j: nc.sync.dma_start(out=qaug[64:68, :, :], in_=self.qaug_in.ap()[j]), writes=[("n_qaug", "p")])
            fw.dma("sp", lambda j=j: nc.sync.dma_start(out=cmk, in_=self.cmk_in.ap()[j]), writes=[("n_cmk", "p")])
            fw.dma("sp", lambda j=j: nc.sync.dma_start(out=fbt, in_=self.fb_in.ap()[j]), writes=[("n_fbt", "p")])
            fw.dma("sp", lambda j=j: nc.sync.dma_start(out=gts, in_=self.gt.ap()[j * 128:(j + 1) * 128, :]), reads=["gt"], writes=[("n_gates", "p")])
            jz = 0 if j == 0 else 1
            win_chunks = []
            for rel in range(8):
                tw = max(4 * j - 4 + rel, 0)
                for g in range(2):
                    fw.dma("sp", lambda g=g, rel=rel, tw=tw: nc.sync.dma_start(out=wKT[g][0:64, rel, :], in_=ksrc(384 + 64 * g, tw)), reads=["kxg"], writes=[("n_kt", "p", 2)])
                    fw.dma("sp", lambda g=g, rel=rel, tw=tw: nc.sync.dma_start(out=wKT[g][64:68, rel, :], in_=self.kaug_in.ap()[:, tw * 128:(tw + 1) * 128]), writes=[("n_kt", "p", 2)])
                    fw.dma("sp", lambda g=g, rel=rel, tw=tw: nc.sync.dma_start(out=wV[g][:, rel, 0:64], in_=vsrc(128 + 64 * g, tw)), reads=["vxg"], writes=[("n_vt", "p", 2)])
                    fw.op("pool", lambda g=g, rel=rel: nc.gpsimd.memset(wV[g][:, rel, 64:65], 1.0), writes=[("n_vt", "p", 2)])
                win_chunks.append(([wKT[0][:, rel, :], wKT[1][:, rel, :]], [wV[0][:, rel, :], wV[1][:, rel, :]], 0,
                                   wm1[:, jz, rel, :].unsqueeze(1).to_broadcast([128, 4, 128]), "n_wm1"))
            slc_chunks = []
            for c in range(4 * j + 4):
                mk = None
                if c >= 4 * j:
                    mk = cm1[:, c - 4 * j, :].unsqueeze(1).to_broadcast([128, 4, 128])
                slc_chunks.append(([sKT[0][:, c, :], sKT[1][:, c, :]], [sV[0][:, c, :], sV[1][:, c, :]], 128 * c, mk, "n_cm1"))
            self.nsa_block(128, 512, 128, qaug, KcT, Vc1, cmk, fbt, gts, slc_chunks, win_chunks, cols, "p")
        self.ps_lim = 8


    def gdn_sample(self, l):
        cfg, nc, fw = self.cfg, self.nc, self.fw
        TP, TL = cfg.TP, cfg.TL
        self.gdn_alloc()
        a = self.aalloc
        xes = a("gs_xes", [128, 16, 11], F32)
        gseq = a("gs_gseq", [128, 16], F32)
        gl16 = a("gs_gl16", [128, 16], F32)
        dlt = a("gs_dlt", [128, 1], F32)
        S0 = [a(f"gs_S0_{i}", [128, 128], F32) for i in range(2)]
        Sn = [a(f"gs_Sn_{i}", [128, 128], F32) for i in range(2)]
        kdm = [a(f"gs_kdm_{i}", [128, 128], F32) for i in range(2)]
        uT = a("gs_uT", [128, 128], F32)
        vnT = a("gs_vnT", [128, 128], F32)
        qdS = a("gs_qdS", [128, 128], F32)
        oTs = a("gs_oTs", [128, 128], F32)
        gate = a("gs_gate", [128, 128], BF16)
        N = 128
        for h in range(4):
            for c in range(3):
                ci = 4 * c + h
                fw.dma("sp", lambda ci=ci: nc.sync.dma_start(out=xes[:, :, 0:3], in_=self.sconv_in.ap()[l, ci]), writes=["gs_xes"])
                self.copy("pool", xes[:, :, 3:11], self.sqkv[:, ci, :].rearrange("p (s t) -> p s t", t=8), reads=[("sqkv", ci)], writes=["gs_xes"])
                y3 = self.g_y[c][:, 0:N].rearrange("p (s t) -> p s t", t=8)
                fw.op("dve", lambda ci=ci, y3=y3: nc.vector.tensor_scalar(out=y3, in0=xes[:, :, 0:8], scalar1=self.cwS[:, l, ci, 0:1], scalar2=None, op0=ALU.mult),
                      reads=["gs_xes", "cwS"], writes=[("g_y", c)])
                for jj in range(1, 4):
                    fw.op("dve", lambda ci=ci, y3=y3, jj=jj: nc.vector.scalar_tensor_tensor(out=y3, in0=xes[:, :, jj:jj + 8], scalar=self.cwS[:, l, ci, jj:jj + 1], in1=y3, op0=ALU.mult, op1=ALU.add),
                          reads=["gs_xes", "cwS", ("g_y", c)], writes=[("g_y", c)])
            fw.lastw[("bsrc", "s")] = fw.lastw.get(("sg8", "s"))
            fw.lastw[("gsrc", "s")] = fw.lastw.get(("gg8", "s"))
            fw.readers[("bsrc", "s")] = []
            fw.readers[("gsrc", "s")] = []
            pT, pTk, pV, pVk = self.gdn_chunk_math(1, 2, 4, 5, 6, self.sg8s, h, self.gg8s, 4 + h, 8, self.g_y, "s")
            fw.op("dve", lambda: nc.vector.tensor_scalar(out=gseq[:, :], in0=self.seqind[:, :], scalar1=self.g_bgc[:, 0, 1:2], scalar2=None, op0=ALU.mult), reads=["seqind", ("g_bgc", 0)], writes=["gs_gseq"])
            pl, plk = self.psb()
            fw.mm([lambda pl=pl: nc.tensor.matmul(pl[:, 0:16], lhsT=self.ones_f[:], rhs=gseq[:, :], start=True, stop=True)], reads=["gs_gseq", "ones_f"], writes=[plk])
            fw.op("act", lambda pl=pl: nc.scalar.activation(out=gl16[:, :], in_=pl[:, 0:16], func=AF.Exp), reads=[plk], writes=["gs_gl16"])
            pl2, pl2k = self.psb()
            fw.mm([lambda pl2=pl2: nc.tensor.matmul(pl2[:, 0:1], lhsT=self.cmask[:, 7, :], rhs=self.g_bgc[:, 0, 1:2], start=True, stop=True)], reads=[("g_bgc", 0), "cmask"], writes=[pl2k])
            fw.op("dve", lambda pl2=pl2: nc.vector.tensor_tensor(out=self.g_ekd[:, 0:1], in0=pl2[:, 0:1], in1=self.g_dc[:, 0:1], op=ALU.subtract), reads=[pl2k, "g_dc"], writes=["g_ekd"])
            fw.op("act", lambda: nc.scalar.activation(out=self.g_ekd[:, 0:1], in_=self.g_ekd[:, 0:1], func=AF.Exp), reads=["g_ekd"], writes=["g_ekd"])
            cs = slice(0, 128)
            fw.op("dve", lambda pT=pT: nc.vector.tensor_scalar(out=self.g_kd[:, cs], in0=pT[:, cs], scalar1=self.g_ekd[:, 0:1], scalar2=None, op0=ALU.mult), reads=[pTk, "g_ekd"], writes=[("g_kd", 0)])
            fw.op("dve", lambda pT=pT: nc.vector.tensor_scalar(out=self.g_Kbd[:, cs], in0=pT[:, cs], scalar1=self.g_bgc[:, 0, 0:1], scalar2=self.g_edc[:, 0:1], op0=ALU.mult, op1=ALU.mult), reads=[pTk, ("g_bgc", 0), "g_edc"], writes=[("g_Kbd", 0)])
            fw.op("act", lambda pV=pV: nc.scalar.activation(out=self.g_Vb[:, cs], in_=pV[:, cs], func=AF.Copy, scale=self.g_bgc[:, 0, 0:1]), reads=[pVk, ("g_bgc", 0)], writes=[("g_Vb", 0)])
            self.gdn_inverse(1, 2)
            pu, puk = self.psb()
            fw.mm([lambda pu=pu: nc.tensor.matmul(pu[:, 0:128], lhsT=self.g_Vb[:, cs], rhs=self.g_M[:, cs], start=True, stop=True)], reads=[("g_M", 0), ("g_Vb", 0)], writes=[puk])
            self.copy("act", uT[:, :], pu[:, 0:128], reads=[puk], writes=["gs_uT"])
            pw, pwk = self.psb()
            fw.mm([lambda pw=pw: nc.tensor.matmul(pw[:, 0:128], lhsT=self.g_Kbd[:, cs], rhs=self.g_M[:, cs], start=True, stop=True)], reads=[("g_M", 0), ("g_Kbd", 0)], writes=[pwk])
            self.copy("dve", self.g_wT[:, cs], pw[:, 0:128], reads=[pwk], writes=["g_wT"])
            pws, pwsk = self.psb()
            pqs, pqsk = self.psb()
            for sq in range(16):
                si = sq % 2
                fw.dma("sp", lambda sq=sq, si=si, h=h: nc.sync.dma_start(out=S0[si][:, :], in_=self.sgdn_in.ap()[l, sq, h]), writes=[("gs_S0", si)])
                fw.mm([lambda sq=sq, si=si, pws=pws: nc.tensor.matmul(pws[:, 8 * sq:8 * sq + 8], lhsT=S0[si][:, :], rhs=self.g_wT[:, 8 * sq:8 * sq + 8], start=True, stop=True)], reads=[("gs_S0", si), "g_wT"], writes=[pwsk])
                fw.mm([lambda sq=sq, si=si, pqs=pqs: nc.tensor.matmul(pqs[:, 8 * sq:8 * sq + 8], lhsT=S0[si][:, :], rhs=self.g_qdT[:, 8 * sq:8 * sq + 8], start=True, stop=True)], reads=[("gs_S0", si), "g_qdT"], writes=[pqsk])
            fw.op("dve", lambda pws=pws: nc.vector.tensor_tensor(out=vnT[:, :], in0=uT[:, :], in1=pws[:, 0:128], op=ALU.subtract), reads=["gs_uT", pwsk], writes=["gs_vnT"])
            self.copy("act", qdS[:, :], pqs[:, 0:128], reads=[pqsk], writes=["gs_qdS"])
            pvn, pvnk = self.psb()
            fw.mm([lambda pvn=pvn: nc.tensor.transpose(pvn[:, 0:128], vnT[:, :], self.cmask[:, 3, :])], reads=["gs_vnT", "cmask"], writes=[pvnk])
            self.copy("dve", self.g_vnew[:, :], pvn[:, 0:128], reads=[pvnk], writes=["g_vnew"])
            po, pok = self.psb()
            fw.mm([lambda po=po: nc.tensor.matmul(po[:, 0:128], lhsT=self.g_vnew[:, :], rhs=self.g_attnT[:, cs], start=True, stop=True)], reads=["g_vnew", ("g_attnT", 0)], writes=[pok])
            fw.op("dve", lambda po=po: nc.vector.tensor_tensor(out=oTs[:, :], in0=qdS[:, :], in1=po[:, 0:128], op=ALU.add), reads=["gs_qdS", pok], writes=["gs_oTs"])
            fw.op("act", lambda: nc.scalar.activation(out=self.g_sq[:, 0:128], in_=oTs[:, :], func=AF.Square), reads=["gs_oTs"], writes=["g_sq"])
            pss, pssk = self.psb()
            fw.mm([lambda pss=pss: nc.tensor.matmul(pss[:, 0:128], lhsT=self.ones_f[:], rhs=self.g_sq[:, 0:128], start=True, stop=True)], reads=["g_sq", "ones_f"], writes=[pssk])
            fw.op("act", lambda pss=pss: nc.scalar.activation(out=self.g_rinv[:, 0:128], in_=pss[:, 0:128], func=AF.Ln, bias=self.epsc[:, 0:1], scale=1.0 / 128), reads=[pssk, "epsc"], writes=["g_rinv"])
            fw.op("act", lambda: nc.scalar.activation(out=self.g_rinv[:, 0:128], in_=self.g_rinv[:, 0:128], func=AF.Exp, scale=-0.5), reads=["g_rinv"], writes=["g_rinv"])
            fw.dma("sp", lambda h=h: nc.sync.dma_start(out=gate[:, :], in_=self.gate_d.ap()[h, :, TP:TL]), reads=["gate_d"], writes=["gs_gate"])
            fw.op("dve", lambda: nc.vector.scalar_tensor_tensor(out=oTs[:, :], in0=oTs[:, :], scalar=self.gnv[:, l:l + 1], in1=self.g_rinv[:, 0:128], op0=ALU.mult, op1=ALU.mult), reads=["gs_oTs", "gnv", "g_rinv"], writes=["gs_oTs"])
            fw.op("dve", lambda h=h: nc.vector.tensor_tensor(out=gate[:, :], in0=oTs[:, :], in1=gate[:, :], op=ALU.mult), reads=["gs_oTs", "gs_gate"], writes=["gs_gate"])
            fw.dma("sp", lambda h=h: nc.sync.dma_start(out=self.o_d.ap()[h, :, TP:TL], in_=gate[:, :]), reads=["gs_gate"], writes=["o_d"])
            for sq in range(16):
                si = sq % 2
                fw.dma("sp", lambda sq=sq, si=si, h=h: nc.sync.dma_start(out=S0[si][:, :], in_=self.sgdn_in.ap()[l, sq, h]), writes=[("gs_S0", si)])
                fw.op("dve", lambda sq=sq, si=si: nc.vector.tensor_scalar(out=kdm[si][:, :], in0=self.g_kd[:, cs], scalar1=self.seqind[:, sq:sq + 1], scalar2=None, op0=ALU.mult), reads=[("g_kd", 0), "seqind"], writes=[("gs_kdm", si)])
                psd, psdk = self.psb()
                fw.mm([lambda si=si, psd=psd: nc.tensor.matmul(psd[:, 0:128], lhsT=kdm[si][:, :], rhs=self.g_vnew[:, :], start=True, stop=True)], reads=[("gs_kdm", si), "g_vnew"], writes=[psdk])
                fw.op("dve", lambda sq=sq, si=si, psd=psd: nc.vector.scalar_tensor_tensor(out=Sn[si][:, :], in0=S0[si][:, :], scalar=gl16[:, sq:sq + 1], in1=psd[:, 0:128], op0=ALU.mult, op1=ALU.add),
                      reads=[("gs_S0", si), "gs_gl16", psdk], writes=[("gs_Sn", si)])
                fw.dma("sp", lambda sq=sq, si=si, h=h: nc.sync.dma_start(out=self.gdnS.ap()[l, sq, h], in_=Sn[si][:, :]), reads=[("gs_Sn", si)], writes=["gdnS_out"])


    def nsa_sample(self, l):
        cfg, nc, fw = self.cfg, self.nc, self.fw
        TP, TL, NPC, PAST = cfg.TP, cfg.TL, cfg.NPG, cfg.PAST
        NCH = NPC + 1
        a = self.aalloc
        self.areset()
        self.ps_lim = 6
        self.psn = 0
        self.nsa_common_alloc(l, 128)
        self.nsa_work_alloc()
        IDf = self.cmask[:, 3, :]
        fw.dma("sp", lambda: nc.sync.dma_start(out=self.winS.ap()[l], in_=self.cwin_in.ap()[l, :, 8:512, :]), writes=["winS_out"])
        ptb = a("ns_ptb", [128, 16 * NPC], I32)
        idx = a("ns_idx", [128, 16 * NPC], I32)
        iot = a("ns_iot", [128, 1], F32)
        fw.dma("sp", lambda: nc.sync.dma_start(out=ptb, in_=self.ptab_in.ap().partition_broadcast(128)), writes=["ns_ptb"])
        fw.dma("sp", lambda: nc.sync.dma_start(out=iot, in_=self.iota_in.ap()), writes=["ns_iot"])
        fw.op("dve", lambda: nc.vector.tensor_scalar(out=idx, in0=ptb, scalar1=128.0, scalar2=iot[:, 0:1], op0=ALU.mult, op1=ALU.add), reads=["ns_ptb", "ns_iot"], writes=["ns_idx"])
        sKT = a("ns_sKT", [68, 2, NCH, 128], BF16)
        sV = a("ns_sV", [128, 2, NCH, 65], BF16)
        raw = a("ns_raw", [64, 4, PAST + 16], BF16)
        wKT = a("ns_wKT", [68, 2, 5, 128], BF16)
        wV = a("ns_wV", [128, 2, 5, 65], BF16)
        KcT = [a(f"ns_KcT{g}", [68, 128], BF16) for g in range(2)]
        Vc1 = [a(f"ns_Vc1{g}", [128, 1, 64], BF16) for g in range(2)]
        pgc = [a(f"ns_pgc{i}", [128, 256], F32) for i in range(2)]
        pgs = [a(f"ns_pgs{i}", [128, 256], F32) for i in range(2)]
        wint = a("ns_wint", [128, 4, 256], F32)
        qaug = a("ns_qaug", [68, 8, 8], BF16)
        cmk = a("ns_cmk", [8, 128], BF16)
        fbt = a("ns_fbt", [8, 64], F32)
        gts = a("ns_gts", [8, 24], F32)
        cms = a("ns_cms", [128, 8], BF16)
        wms = a("ns_wms", [128, 5, 8], BF16)
        fw.op("pool", lambda: nc.gpsimd.memset(sKT, 0.0), writes=[("n_kt", "s", 1)])
        fw.op("pool", lambda: nc.gpsimd.memset(sV, 0.0), writes=[("n_vt", "s", 1)])
        fw.op("pool", lambda: nc.gpsimd.memset(wKT, 0.0), writes=[("n_kt", "s", 2)])
        fw.op("pool", lambda: nc.gpsimd.memset(wV, 0.0), writes=[("n_vt", "s", 2)])
        fw.op("pool", lambda: nc.gpsimd.memset(sV[:, :, :, 64:65], 1.0), writes=[("n_vt", "s", 1)])
        fw.op("pool", lambda: nc.gpsimd.memset(wV[:, :, :, 64:65], 1.0), writes=[("n_vt", "s", 2)])
        fw.op("pool", lambda: nc.gpsimd.memset(raw[:, :, PAST:PAST + 16], 0.0), writes=[("n_raw", "s", 0), ("n_raw", "s", 1)])
        for g in range(2):
            fw.dma("sp", lambda g=g: nc.sync.dma_start(out=sKT[64:68, g, :, :], in_=self.kaug_in.ap()[:, 0:NCH * 128].rearrange("r (t c) -> r t c", c=128)), writes=[("n_kt", "s", 1)])
            fw.dma("sp", lambda g=g: nc.sync.dma_start(out=wKT[64:68, g, :, :], in_=self.kaug_in.ap()[:, (NPC - 4) * 128:(NPC + 1) * 128].rearrange("r (t c) -> r t c", c=128)), writes=[("n_kt", "s", 2)])
            fw.dma("sp", lambda g=g: nc.sync.dma_start(out=KcT[g][64:68, :], in_=self.caug_in.ap()[:, 0:128]), writes=[("n_KcT", g)])
            fw.op("pool", lambda g=g: nc.gpsimd.memset(KcT[g][0:64, :], 0.0), writes=[("n_KcT", g)])
            fw.op("pool", lambda g=g: nc.gpsimd.memset(Vc1[g][:, :, :], 0.0), writes=[("n_Vc1", g)])
        fw.dma("sp", lambda: nc.sync.dma_start(out=qaug[64:68, :, :], in_=self.qaugs_in.ap()), writes=[("n_qaug", "s")])
        fw.dma("sp", lambda: nc.sync.dma_start(out=cmk, in_=self.cmks_in.ap()), writes=[("n_cmk", "s")])
        fw.dma("sp", lambda: nc.sync.dma_start(out=fbt, in_=self.fbs_in.ap()), writes=[("n_fbt", "s")])
        fw.dma("sp", lambda: nc.sync.dma_start(out=cms, in_=self.cms_in.ap()), writes=["ns_cms"])
        fw.dma("sp", lambda: nc.sync.dma_start(out=wms, in_=self.wms_in.ap()), writes=["ns_wms"])
        nblk = PAST // 16 - 1
        poolrows = cfg.NPOOL * 128
        for sq in range(16):
            for pg in range(NPC):
                bi_ = (sq * NPC + pg) % 2
                col = sq * NPC + pg
                fw.dma("pool", lambda bi_=bi_, col=col: nc.gpsimd.indirect_dma_start(out=pgc[bi_][:, :], out_offset=None, in_=self.ccmp_in.ap(),
                       in_offset=bass.IndirectOffsetOnAxis(ap=idx[:, col:col + 1], axis=0), element_offset=l * poolrows * 256), reads=["ns_idx"], writes=[("ns_pgc", bi_)])
                fw.dma("pool", lambda bi_=bi_, col=col: nc.gpsimd.indirect_dma_start(out=pgs[bi_][:, :], out_offset=None, in_=self.cslc_in.ap(),
                       in_offset=bass.IndirectOffsetOnAxis(ap=idx[:, col:col + 1], axis=0), element_offset=l * poolrows * 256), reads=["ns_idx"], writes=[("ns_pgs", bi_)])
                pt, pk = self.psb()
                for cg in range(4):
                    fw.mm([lambda cg=cg, pt=pt, bi_=bi_: nc.tensor.transpose(pt[0:64, cg * 128:(cg + 1) * 128], pgc[bi_][:, cg * 64:(cg + 1) * 64], IDf)], reads=[("ns_pgc", bi_), "cmask"], writes=[pk])
                self.copy("act", raw[:, :, pg * 128:(pg + 1) * 128], pt[0:64, 0:512].rearrange("p (a b) -> p a b", b=128), reads=[pk], writes=[("n_raw", "s", 0), ("n_raw", "s", 1)])
                pt, pk = self.psb()
                for g in range(2):
                    fw.mm([lambda g=g, pt=pt, bi_=bi_: nc.tensor.transpose(pt[0:64, g * 128:(g + 1) * 128], pgs[bi_][:, g * 64:(g + 1) * 64], IDf)], reads=[("ns_pgs", bi_), "cmask"], writes=[pk])
                self.copy("dve", sKT[0:64, :, pg, :], pt[0:64, 0:256].rearrange("p (a b) -> p a b", b=128), reads=[pk], writes=[("n_kt", "s", 1)])
                self.copy("pool", sV[:, :, pg, 0:64], pgs[bi_][:, 128:256].rearrange("p (a b) -> p a b", b=64), reads=[("ns_pgs", bi_)], writes=[("n_vt", "s", 1)])
            fw.dma("sp", lambda sq=sq: nc.sync.dma_start(out=wint, in_=self.cwin_in.ap()[l, sq].rearrange("(r t) f -> t r f", t=128)), writes=["ns_wint"])
            for rel in range(4):
                pt, pk = self.psb()
                for g in range(2):
                    fw.mm([lambda g=g, pt=pt, rel=rel: nc.tensor.transpose(pt[0:64, g * 128:(g + 1) * 128], wint[:, rel, g * 64:(g + 1) * 64], IDf)], reads=["ns_wint", "cmask"], writes=[pk])
                self.copy("act", wKT[0:64, :, rel, :], pt[0:64, 0:256].rearrange("p (a b) -> p a b", b=128), reads=[pk], writes=[("n_kt", "s", 2)])
                self.copy("pool", wV[:, :, rel, 0:64], wint[:, rel, 128:256].rearrange("p (a b) -> p a b", b=64), reads=["ns_wint"], writes=[("n_vt", "s", 2)])
            for g in range(2):
                fw.dma("sp", lambda g=g, sq=sq: nc.sync.dma_start(out=sKT[0:64, g, NPC, 0:8], in_=self.kxs.ap()[256 + 64 * g:320 + 64 * g, 8 * sq:8 * sq + 8]), reads=["kxs"], writes=[("n_kt", "s", 1)])
                fw.dma("sp", lambda g=g, sq=sq: nc.sync.dma_start(out=wKT[0:64, g, 4, 0:8], in_=self.kxs.ap()[384 + 64 * g:448 + 64 * g, 8 * sq:8 * sq + 8]), reads=["kxs"], writes=[("n_kt", "s", 2)])
                fw.dma("sp", lambda g=g, sq=sq: nc.sync.dma_start(out=sV[0:8, g, NPC, 0:64], in_=self.vxs.ap()[8 * sq:8 * sq + 8, 64 * g:64 * g + 64]), reads=["vxs"], writes=[("n_vt", "s", 1)])
                fw.dma("sp", lambda g=g, sq=sq: nc.sync.dma_start(out=wV[0:8, g, 4, 0:64], in_=self.vxs.ap()[8 * sq:8 * sq + 8, 128 + 64 * g:192 + 64 * g]), reads=["vxs"], writes=[("n_vt", "s", 2)])
            fw.dma("sp", lambda sq=sq: nc.sync.dma_start(out=qaug[0:64, :, :], in_=self.qx.ap()[:, :, TP + 8 * sq:TP + 8 * sq + 8].rearrange("h d c -> d h c")), reads=["qx"], writes=[("n_qaug", "s")])
            fw.dma("sp", lambda sq=sq: nc.sync.dma_start(out=gts, in_=self.gt.ap()[TP + 8 * sq:TP + 8 * sq + 8, :]), reads=["gt"], writes=[("n_gates", "s")])
            for g in range(2):
                self.nsa_compress(g, raw[:, g, :], raw[:, 2 + g, :], nblk, KcT[g], Vc1[g], "s")
            slc_chunks = []
            for c in range(NCH):
                mk = cms[:, :].unsqueeze(1).to_broadcast([128, 4, 8]) if c == NPC else None
                slc_chunks.append(([sKT[:, 0, c, :], sKT[:, 1, c, :]], [sV[:, 0, c, :], sV[:, 1, c, :]], 128 * c, mk, "ns_cms"))
            win_chunks = []
            for rel in range(5):
                mk = wms[:, rel, :].unsqueeze(1).to_broadcast([128, 4, 8]) if rel in (0, 4) else None
                win_chunks.append(([wKT[:, 0, rel, :], wKT[:, 1, rel, :]], [wV[:, 0, rel, :], wV[:, 1, rel, :]], 0, mk, "ns_wms"))
            self.nsa_block(8, 128, 64, qaug, KcT, Vc1, cmk, fbt, gts, slc_chunks, win_chunks, slice(TP + 8 * sq, TP + 8 * sq + 8), "s")
        self.ps_lim = 8

    def phase_c(self, l):
        cfg, nc, fw = self.cfg, self.nc, self.fw
        TL = cfg.TL
        self.areset()
        self.sq = self.aalloc("sq_sb", [128, KC, 512], BF16)
        self.rstd = self.aalloc("rstd_sb", [128, 512], F32)
        self.wt = [self.aalloc(f"wt{i}", [128, KC, 512], BF16) for i in range(2)]
        self.actT = self.aalloc("actT_sb", [128, FC, 768], BF16)
        self.sg = [self.aalloc(f"sg{i}", [128, 512], F32) for i in range(2)]
        self.w2t = [self.aalloc(f"w2t{i}", [128, FC, 128], BF16) for i in range(2)]
        self.sgn = 0
        blocks = [(c0, min(512, TL - c0)) for c0 in range(0, TL, 512)]
        self.oT = self.hT
        for k in range(KC):
            fw.dma("sp", lambda k=k: nc.sync.dma_start(out=self.hT[:, k, :], in_=self.o_d.ap()[k]), reads=["o_d"], writes=[("hT", k)])
        for half in range(2):
            wi = self.wtn % 2
            self.wtn += 1
            wt = self.wt[wi]
            fw.dma("sp", lambda half=half, wt=wt: nc.sync.dma_start(out=wt[:, :, 0:512], in_=self.wb_out.ap()[l, :, :, half * 512:(half + 1) * 512]),
                   reads=["wb"], writes=[("wt", wi)])
            for oc in range(4):
                ko = half * 4 + oc
                for (c0, n) in blocks:
                    pt, pk = self.psb()
                    fw.mm([lambda k=k, pt=pt, n=n, oc=oc, c0=c0, wt=wt: nc.tensor.matmul(
                        pt[:, 0:n], lhsT=wt[:, k, oc * 128:(oc + 1) * 128], rhs=self.oT[:, k, c0:c0 + n], start=(k == 0), stop=(k == KC - 1))
                        for k in range(KC)], reads=[("wt", wi)] + [("hT", k) for k in range(KC)], writes=[pk])
                    fw.op("dve", lambda pt=pt, n=n, ko=ko, c0=c0: nc.vector.tensor_tensor(
                        out=self.xT[:, ko, c0:c0 + n], in0=self.xT[:, ko, c0:c0 + n], in1=pt[:, 0:n], op=ALU.add),
                        reads=[pk, ("xT", ko)], writes=[("xT", ko)])
        self.rmsnorm_to(l * 2 * KC + KC, self.hT, "hT")
        halves = [(h0, min(768, TL - h0)) for h0 in range(0, TL, 768)]
        for (h0, hn) in halves:
            hblocks = [(c0, min(512, h0 + hn - c0)) for c0 in range(h0, h0 + hn, 512)]
            for fc in range(FC):
                wi = self.wtn % 2
                self.wtn += 1
                wt = self.wt[wi]
                fw.dma("sp", lambda fc=fc, wt=wt: nc.sync.dma_start(out=wt[:, :, 0:128], in_=self.wb_f1.ap()[l, :, :, fc * 128:(fc + 1) * 128]),
                       reads=["wb"], writes=[("wt", wi)])
                fw.dma("pool", lambda fc=fc, wt=wt: nc.gpsimd.dma_start(out=wt[:, :, 128:256], in_=self.wb_f1.ap()[l, :, :, D_FF + fc * 128:D_FF + (fc + 1) * 128]),
                       reads=["wb"], writes=[("wt", wi)])
                for (c0, n) in hblocks:
                    pg, pgk = self.psb()
                    fw.mm([lambda k=k, pg=pg, n=n, c0=c0, wt=wt: nc.tensor.matmul(
                        pg[:, 0:n], lhsT=wt[:, k, 0:128], rhs=self.hT[:, k, c0:c0 + n], start=(k == 0), stop=(k == KC - 1))
                        for k in range(KC)], reads=[("wt", wi)] + [("hT", k) for k in range(KC)], writes=[pgk])
                    pu, puk = self.psb()
                    fw.mm([lambda k=k, pu=pu, n=n, c0=c0, wt=wt: nc.tensor.matmul(
                        pu[:, 0:n], lhsT=wt[:, k, 128:256], rhs=self.hT[:, k, c0:c0 + n], start=(k == 0), stop=(k == KC - 1))
                        for k in range(KC)], reads=[("wt", wi)] + [("hT", k) for k in range(KC)], writes=[puk])
                    si = self.sgn % 2
                    self.sgn += 1
                    sg = self.sg[si]
                    fw.op("act", lambda pg=pg, n=n, sg=sg: nc.scalar.activation(out=sg[:, 0:n], in_=pg[:, 0:n], func=AF.Silu),
                          reads=[pgk], writes=[("sg", si)])
                    fw.op("dve", lambda pu=pu, n=n, sg=sg, fc=fc, c0=c0, h0=h0: nc.vector.tensor_tensor(
                        out=self.actT[:, fc, c0 - h0:c0 - h0 + n], in0=sg[:, 0:n], in1=pu[:, 0:n], op=ALU.mult),
                        reads=[puk, ("sg", si)], writes=[("actT", fc)])
            for ko in range(KC):
                wi = self.wtn % 2
                self.wtn += 1
                w2 = self.w2t[wi]
                fw.dma("sp", lambda ko=ko, w2=w2: nc.sync.dma_start(out=w2[:, :, :], in_=self.wb_f2.ap()[l, :, :, ko * 128:(ko + 1) * 128]),
                       reads=["wb"], writes=[("w2t", wi)])
                for (c0, n) in hblocks:
                    pt, pk = self.psb()
                    fw.mm([lambda k=k, pt=pt, n=n, c0=c0, w2=w2, h0=h0: nc.tensor.matmul(
                        pt[:, 0:n], lhsT=w2[:, k, :], rhs=self.actT[:, k, c0 - h0:c0 - h0 + n], start=(k == 0), stop=(k == FC - 1))
                        for k in range(FC)], reads=[("w2t", wi)] + [("actT", k) for k in range(FC)], writes=[pk])
                    fw.op("dve", lambda pt=pt, n=n, ko=ko, c0=c0: nc.vector.tensor_tensor(
                        out=self.xT[:, ko, c0:c0 + n], in0=self.xT[:, ko, c0:c0 + n], in1=pt[:, 0:n], op=ALU.add),
                        reads=[pk, ("xT", ko)], writes=[("xT", ko)])

    def final_norm(self):
        cfg, nc, fw = self.cfg, self.nc, self.fw
        L = cfg.DEPTH
        self.areset()
        self.sq = self.aalloc("sq_sb", [128, KC, 512], BF16)
        self.rstd = self.aalloc("rstd_sb", [128, 512], F32)
        self.rmsnorm_to(L * 2 * KC, self.xT, "xT")
        for k in range(KC):
            fw.dma("sp", lambda k=k: nc.sync.dma_start(out=self.yT.ap()[k * 128:(k + 1) * 128, :], in_=self.xT[:, k, :]),
                   reads=[("xT", k)], writes=["yT_out"])


def _local_tokens_T(x_prompt, x_sample, cfg, c):
    b, r = c // 4, c % 4
    xp = x_prompt[b].reshape(cfg.NT, 128, D_MODEL)[r::4].reshape(cfg.TP, D_MODEL)
    xs = x_sample[16 * c:16 * c + 16].reshape(128, D_MODEL)
    return np.ascontiguousarray(np.concatenate([xp, xs], 0).T)


def run(cfg, inputs, trace=False):
    L = cfg.DEPTH
    bld = Builder(cfg)
    bld.convS = None
    nc = bld.nc
    bld.convS = bld.dram("convS", [L, GDN_QKV, 16, 3], F32, "ExternalOutput")
    bld.build()
    f32 = np.float32
    gvec = np.zeros((128, L * 2 * KC + KC), f32)
    for l in range(L):
        gvec[:, l * 16:l * 16 + 8] = inputs["norm_mix"][l].reshape(KC, 128).T
        gvec[:, l * 16 + 8:l * 16 + 16] = inputs["norm_ffn"][l].reshape(KC, 128).T
    gvec[:, L * 16:] = inputs["norm_final"].reshape(KC, 128).T
    NJ = cfg.NJ
    ar = np.arange(128)
    cmask = np.zeros((128, 8, 128), f32)
    seq = ar // 8
    same = seq[:, None] == seq[None, :]
    cmask[:, 0, :] = (ar[:, None] <= ar[None, :])
    cmask[:, 1, :] = np.where(ar[None, :] >= ar[:, None], 0.0, NEG)
    cmask[:, 2, :] = (ar[None, :] > ar[:, None])
    cmask[:, 3, :] = np.eye(128)
    cmask[:, 4, :] = (ar[:, None] <= ar[None, :]) & same
    cmask[:, 5, :] = np.where((ar[None, :] >= ar[:, None]) & same, 0.0, NEG)
    cmask[:, 6, :] = (ar[None, :] > ar[:, None]) & same
    cmask[:, 7, :] = same
    seqind = (seq[:, None] == np.arange(16)[None, :]).astype(f32)
    mmask = np.zeros((128, 4, 128), f32)
    for lv, sz in enumerate((16, 32, 64, 128)):
        mmask[:, lv, :] = (ar[:, None] > ar[None, :]) & ((ar[:, None] // sz) == (ar[None, :] // sz)) & ((ar[:, None] // (sz // 2)) != (ar[None, :] // (sz // 2)))
    selrows = np.zeros((8, 8, 128), f32)
    for r in range(8):
        selrows[r, r, :] = 1.0
    cw = inputs["conv_w"]
    cwS = np.ascontiguousarray(np.transpose(cw.reshape(L, 4, 12, 128), (3, 0, 2, 1)))
    gba = np.zeros((8, L, 2), f32)
    gba[4:8, :, 0] = inputs["gdn_dt_bias"].T
    gba[4:8, :, 1] = inputs["gdn_a_log"].T
    gnv = np.ascontiguousarray(inputs["gdn_norm"].T)
    bf = ml_dtypes.bfloat16
    NT = cfg.NT
    npos = 128 * max(NT, 17)
    pos = np.arange(npos)
    Gtab = (np.arange(128)[:, None] == (pos[None, :] // 64)).astype(bf)
    kaug = np.stack([np.ones(NT * 128), np.ones(NT * 128), 128.0 * (np.arange(NT * 128) // 128), np.arange(NT * 128) % 128.0]).astype(bf)
    nn = np.arange(512)
    caug = np.stack([np.ones(512), np.ones(512), 128.0 * (nn // 8), 16.0 * (nn % 8) + 15.5]).astype(bf)
    slopes = np.array([2.0 ** (-(h + 1)) for h in range(8)])
    cmpw = np.ascontiguousarray(np.transpose(inputs["nsa_cmp_w"], (3, 0, 1, 2, 4)).reshape(64, L, 64, 64))
    cmppe = np.ascontiguousarray(np.transpose(inputs["nsa_cmp_pe"], (3, 0, 1, 2)).reshape(64, L, 64))
    nsag = np.ascontiguousarray(np.broadcast_to(inputs["nsa_norm"][None, :, :], (128, L, 64))).astype(f32)
    ql = np.arange(128)
    PAST = cfg.PAST
    tau = np.arange(8)
    qaugs = np.zeros((4, 8, 8), f32)
    for h in range(8):
        qaugs[0, h, :] = -slopes[h] * 128.0 * (PAST // 128)
        qaugs[1, h, :] = -slopes[h] * tau
        qaugs[2, h, :] = slopes[h]
        qaugs[3, h, :] = slopes[h]
    qaugs = qaugs.astype(bf)
    n128 = np.arange(128)
    cmks = np.where(((16 * n128[None, :] + 31) <= (PAST + tau)[:, None]) & (n128[None, :] < PAST // 16), 0.0, NEG).astype(bf)
    curs = PAST // 64
    b64 = np.arange(64)
    fbs = np.broadcast_to(np.where(b64 <= curs, np.where((b64 == 0) | (b64 == curs) | (b64 == curs - 1), 1e4, 0.0), -1e30)[None, :], (8, 64)).astype(f32).copy()
    cms1 = np.where((ql[:, None] <= tau[None, :]) & (ql[:, None] < 8), 0.0, NEG)
    cms = cms1.astype(bf)
    wms = np.zeros((128, 5, 8), f32)
    wms[:, 0] = np.where(ql[:, None] > tau[None, :], 0.0, NEG)
    wms[:, 4] = cms1
    wms = wms.astype(bf)
    in_maps = []
    for c in range(8):
        b, r = c // 4, c % 4
        i = r
        qaug = np.zeros((NJ, 4, 8, 128), f32)
        cmk = np.zeros((NJ, 128, 512), f32)
        fbt = np.zeros((NJ, 128, 128), f32)
        for j in range(NJ):
            t = 4 * j + r
            p_q = 128 * t + ql
            for h in range(8):
                qaug[j, 0, h, :] = -slopes[h] * 128.0 * t
                qaug[j, 1, h, :] = -slopes[h] * ql
                qaug[j, 2, h, :] = slopes[h]
                qaug[j, 3, h, :] = slopes[h]
            okc = ((16 * nn[None, :] + 31) <= p_q[:, None]) & (nn[None, :] < NT * 8 - 1)
            cmk[j] = np.where(okc, 0.0, NEG)
            cur = p_q // 64
            blk = np.arange(128)
            forced = (blk[None, :] == 0) | (blk[None, :] == cur[:, None]) | (blk[None, :] == cur[:, None] - 1)
            fbt[j] = np.where(blk[None, :] <= cur[:, None], np.where(forced, 1e4, 0.0), -1e30)
        cmt = np.zeros((128, 4, 128), f32)
        for rel in range(4):
            if rel == r:
                cmt[:, rel, :] = np.where(ql[:, None] <= ql[None, :], 0.0, NEG)
            elif rel > r:
                cmt[:, rel, :] = NEG
        wmt = np.zeros((128, 2, 8, 128), f32)
        for jz in range(2):
            for rel in range(8):
                if rel == r:
                    m = np.where(ql[:, None] > ql[None, :], 0.0, NEG)
                elif r < rel < r + 4:
                    m = np.zeros((128, 128))
                elif rel == r + 4:
                    m = np.where(ql[:, None] <= ql[None, :], 0.0, NEG)
                else:
                    m = np.full((128, 128), NEG)
                if jz == 0 and rel < 4:
                    m = np.full((128, 128), NEG)
                wmt[:, jz, rel, :] = m
        cwH = np.ascontiguousarray(cwS[:, :, [i, 4 + i, 8 + i], :])
        idxtab = np.zeros((128, 8), np.int32)
        idxtab[:, 0] = 128 * i + ar
        idxtab[:, 1] = 512 + 128 * i + ar
        idxtab[:, 2] = 1024 + 128 * i + ar
        idxtab[:, 3] = 1536 + 2 * i + ar
        for h in range(4):
            idxtab[:, 4 + h] = h * 512 + r * 128 + ar
        m = {"xT": _local_tokens_T(inputs["x_prompt"], inputs["x_sample"], cfg, c),
             "w_in": inputs["w_in"], "w_out": inputs["w_out"], "w_ffn_in": inputs["w_ffn_in"],
             "w_ffn_out": inputs["w_ffn_out"], "gvec": gvec,
             "cmask": cmask, "seqind": seqind, "mmask": mmask, "selrows": selrows, "cwH": cwH, "cwS": cwS, "gba": gba,
             "gnv": gnv, "idxtab": idxtab,
             "sgdn": np.ascontiguousarray(inputs["state_gdn"][:, 16 * c:16 * c + 16]),
             "sconv": np.ascontiguousarray(np.transpose(inputs["state_conv"][:, 16 * c:16 * c + 16].reshape(L, 16, 3, 12, 128), (0, 3, 4, 1, 2))),
             "ptab": np.ascontiguousarray(inputs["page_table"][16 * c:16 * c + 16].reshape(1, -1)).astype(np.int32),
             "iotac": ar.reshape(128, 1).astype(f32),
             "cache_cmp": inputs["cache_cmp_kv"].reshape(-1, 256), "cache_slc": inputs["cache_slc_kv"].reshape(-1, 256),
             "cache_win": np.ascontiguousarray(inputs["cache_win_kv"][:, 16 * c:16 * c + 16].reshape(L, 16, 512, 256)),
             "qaugs": qaugs, "cmks": cmks, "fbs": fbs, "cms": cms, "wms": wms,
             "Gtab": Gtab, "kaug": kaug, "caug": caug, "qaug": qaug.astype(bf), "cmk": cmk.astype(bf), "fbt": fbt,
             "cmt": cmt.astype(bf), "wmt": wmt.astype(bf), "cmpw": cmpw, "cmppe": cmppe, "nsag": nsag}
        in_maps.append(m)
    res = run_bass_kernel_spmd(nc, in_maps, core_ids=list(range(8)), trace=trace)
    return res, bld


def assemble(cfg, res):
    L = cfg.DEPTH
    B = 2
    SEQ, NT, TP = cfg.SEQ, cfg.NT, cfg.TP
    f32 = np.float32
    y_p = np.zeros((B, SEQ, D_MODEL), f32)
    y_s = np.zeros((128, 8, D_MODEL), f32)
    kv_p = np.zeros((L, B, SEQ, 768), f32)
    kv_s = np.zeros((L, 128, 8, 768), f32)
    conv_p = np.zeros((L, B, 3, GDN_QKV), f32)
    conv_s = np.zeros((L, 128, 3, GDN_QKV), f32)
    for c in range(8):
        b, r = c // 4, c % 4
        o = res.results[c]
        yT = o["yT"]
        y_p[b].reshape(NT, 128, D_MODEL)[r::4] = yT[:, :TP].T.reshape(cfg.NJ, 128, D_MODEL)
        y_s[16 * c:16 * c + 16] = yT[:, TP:].T.reshape(16, 8, D_MODEL)
        kvT = o["kvT"]
        for l in range(L):
            kv_p[l, b].reshape(NT, 128, 768)[r::4] = kvT[l][:, :TP].T.reshape(cfg.NJ, 128, 768)
            kv_s[l, 16 * c:16 * c + 16] = kvT[l][:, TP:].T.reshape(16, 8, 768)
            if r == 3:
                conv_p[l, b] = o["convT"][l][:, cfg.NJ - 1, :].T
            conv_s[l, 16 * c:16 * c + 16] = np.transpose(o["convS"][l], (1, 2, 0))
    gdn_p = np.zeros((L, B, 4, 128, 128), f32)
    for c in range(8):
        b, r = c // 4, c % 4
        o = res.results[c]
        gdn_p[:, b, r] = o["gdnP"]
    gdn_s = np.concatenate([res.results[c]["gdnS"] for c in range(8)], axis=1)
    win_old = np.concatenate([res.results[c]["winS"] for c in range(8)], axis=1)
    return dict(y_p=y_p, y_s=y_s, kv_p=kv_p, kv_s=kv_s, conv_p=conv_p, conv_s=conv_s, gdn_p=gdn_p, gdn_s=gdn_s, win_old=win_old)


def kernel(**inputs):
    inputs = {k: np.asarray(v) for k, v in inputs.items()}
    B, SEQ, _ = inputs["x_prompt"].shape
    L = inputs["w_in"].shape[0]
    PAST = inputs["page_table"].shape[1] * 128
    cfg = Cfg(SEQ=SEQ, DEPTH=L, PAST=PAST)
    res, bld = run(cfg, inputs)
    o = assemble(cfg, res)
    f32 = np.float32
    kvp, kvs = o["kv_p"], o["kv_s"]
    sh = (2, 2, 64)
    cmp_p = kvp[..., 0:256].reshape(L, B, SEQ, *sh)
    slc_p = kvp[..., 256:512].reshape(L, B, SEQ, *sh)
    win_p = np.ascontiguousarray(kvp[:, :, SEQ - 512:, 512:768]).reshape(L, B, 512, *sh)
    cmp_s = kvs[..., 0:256].reshape(L, 128, 8, *sh)
    slc_s = kvs[..., 256:512].reshape(L, 128, 8, *sh)
    win_s = np.concatenate([o["win_old"], kvs[..., 512:768].reshape(L, 128, 8, 256)], axis=2).reshape(L, 128, 512, *sh)
    outs = (o["y_p"], o["y_s"], cmp_p, cmp_s, slc_p, slc_s, win_p, win_s, o["gdn_p"], o["gdn_s"], o["conv_p"], o["conv_s"])
    return tuple(np.ascontiguousarray(x, dtype=f32) for x in outs)
```
